# Optimizing a Trainium2 kernel written in Bass

```python
import math
import jax, jax.numpy as jnp
from jax import lax
import numpy as np

D_MODEL = 1024
BATCH = 2
SEQ = 8192
DEPTH = 4
DEC_BATCH = 128
DEC_SEQ = 1
PAST_LEN = 2048
PAGE_SIZE = 128

N_A_LAYERS = DEPTH // 2
N_B_LAYERS = DEPTH - N_A_LAYERS
POOL_WINDOWS = (2, 4, 8, 16)
N_POOL_GROUPS = len(POOL_WINDOWS)
POOL_WIDTH = D_MODEL
POOL_GROUP_WIDTH = POOL_WIDTH // N_POOL_GROUPS
POOL_STATE = max(POOL_WINDOWS) - 1
ATTN_PATTERNS = ((128, 1), (512, 4), (2048, 16))
N_GROUPS = len(ATTN_PATTERNS)
HEAD_DIM = 64
HEADS_PER_GROUP = D_MODEL // HEAD_DIM
ATTN_WIDTH = HEADS_PER_GROUP * HEAD_DIM
ROT_DIM = HEAD_DIM // 4
ROPE_THETA = 500000.0
SUB_BLOCK = 128
RMS_EPS = 1e-6

kernel_name = 'yoco_pool_dilated_attn_step'


def rmsnorm(x, g):
    xf = x.astype(jnp.float32)
    y = xf * lax.rsqrt(jnp.mean(xf * xf, axis=-1, keepdims=True) + RMS_EPS)
    return (y * g.astype(jnp.float32)).astype(x.dtype)


def rope_partial(x, pos):
    half = ROT_DIM // 2
    inv = 1.0 / (ROPE_THETA ** (jnp.arange(0, ROT_DIM, 2, dtype=jnp.float32) / ROT_DIM))
    ang = pos.astype(jnp.float32)[:, None] * inv[None, :]
    cos = jnp.cos(ang)[None, :, None, :]
    sin = jnp.sin(ang)[None, :, None, :]
    xr = x[..., :ROT_DIM].astype(jnp.float32)
    x1, x2 = xr[..., :half], xr[..., half:]
    rot = jnp.concatenate([x1 * cos - x2 * sin, x1 * sin + x2 * cos], axis=-1).astype(x.dtype)
    return jnp.concatenate([rot, x[..., ROT_DIM:]], axis=-1)


def pool_residual(u, pos):
    uf = u.astype(jnp.float32)
    c = jnp.cumsum(uf, axis=1)
    L = u.shape[1]
    outs = []
    for gi, w in enumerate(POOL_WINDOWS):
        sl = slice(gi * POOL_GROUP_WIDTH, (gi + 1) * POOL_GROUP_WIDTH)
        cg = c[..., sl]
        prev = jnp.pad(cg, ((0, 0), (w, 0), (0, 0)))[:, :L]
        cnt = jnp.minimum(pos + 1, w).astype(jnp.float32)[None, :, None]
        outs.append((cg - prev) / cnt - uf[..., sl])
    return jnp.concatenate(outs, axis=-1)


def pool_in(h, g, w_in):
    proj = rmsnorm(h, g) @ w_in
    return proj[..., :POOL_WIDTH], proj[..., POOL_WIDTH:]


def pool_out(r, gate, w_grp, scale, w_out, dt):
    B, L, _ = r.shape
    rg = r.astype(dt).reshape(B, L, N_POOL_GROUPS, POOL_GROUP_WIDTH)
    z = jnp.einsum('blgc,gcd->blgd', rg, w_grp, preferred_element_type=jnp.float32).reshape(B, L, POOL_WIDTH)
    y = z * scale.astype(jnp.float32) * jax.nn.silu(gate.astype(jnp.float32))
    return y.astype(dt) @ w_out


def shared_kv(h, g, w_kv, pos):
    B, T, _ = h.shape
    kv = (rmsnorm(h, g) @ w_kv).reshape(B, T, N_GROUPS, 2, HEADS_PER_GROUP, HEAD_DIM)
    k = rope_partial(kv[:, :, :, 0].reshape(B, T, N_GROUPS * HEADS_PER_GROUP, HEAD_DIM), pos)
    k = k.reshape(B, T, N_GROUPS, HEADS_PER_GROUP, HEAD_DIM)
    return jnp.stack([k, kv[:, :, :, 1]], axis=3)


def attn_in(h, g, w_in, pos):
    B, T, _ = h.shape
    proj = rmsnorm(h, g) @ w_in
    q = proj[..., :N_GROUPS * ATTN_WIDTH].reshape(B, T, N_GROUPS * HEADS_PER_GROUP, HEAD_DIM)
    q = rope_partial(q, pos).reshape(B, T, N_GROUPS, HEADS_PER_GROUP, HEAD_DIM)
    return q, proj[..., N_GROUPS * ATTN_WIDTH:]


def dilated_attn_prompt(q, k, v, window, dil):
    B, S, H, Dh = q.shape
    n = window // dil
    Ls = S // dil
    nblk = -(-Ls // SUB_BLOCK)
    Lp = nblk * SUB_BLOCK
    scale = 1.0 / math.sqrt(Dh)

    def to_sub(x):
        x = x.reshape(B, Ls, dil, H, Dh).transpose(0, 2, 1, 3, 4)
        x = jnp.pad(x, ((0, 0), (0, 0), (0, Lp - Ls), (0, 0), (0, 0)))
        return x.reshape(B, dil, nblk, SUB_BLOCK, H, Dh)

    def with_prev(x):
        prev = jnp.pad(x, ((0, 0), (0, 0), (1, 0), (0, 0), (0, 0), (0, 0)))[:, :, :-1]
        return jnp.concatenate([prev, x], axis=3)

    qs = to_sub(q)
    kb = with_prev(to_sub(k))
    vb = with_prev(to_sub(v))
    s = jnp.einsum('brnqhd,brnkhd->brnhqk', qs, kb, preferred_element_type=jnp.float32) * scale
    a = jnp.arange(SUB_BLOCK)[:, None]
    c = jnp.arange(2 * SUB_BLOCK)[None, :]
    delta = SUB_BLOCK + a - c
    key_sub = (jnp.arange(nblk)[:, None, None] - 1) * SUB_BLOCK + c[None]
    valid = (delta >= 0)[None] & (delta <= n)[None] & (key_sub >= 0)
    s = jnp.where(valid[None, None, :, None], s, -jnp.inf)
    m = jnp.max(s, axis=-1)
    p = jnp.exp(s - m[..., None])
    l = jnp.sum(p, axis=-1)
    o = jnp.einsum('brnhqk,brnkhd->brnqhd', p, vb.astype(jnp.float32)) / jnp.swapaxes(l, 3, 4)[..., None]
    lse = jnp.swapaxes(m + jnp.log(l), 3, 4)
    o = o.reshape(B, dil, Lp, H, Dh)[:, :, :Ls].transpose(0, 2, 1, 3, 4).reshape(B, S, H, Dh)
    lse = lse.reshape(B, dil, Lp, H)[:, :, :Ls].transpose(0, 2, 1, 3).reshape(B, S, H)
    return o, lse


def gather_rows(cache, new, idx):
    Lc = cache.shape[1]
    T = new.shape[1]
    from_cache = jnp.take(cache, jnp.clip(idx, 0, Lc - 1), axis=1)
    from_new = jnp.take(new, jnp.clip(idx - Lc, 0, T - 1), axis=1)
    return jnp.where((idx < Lc)[None, :, :, None, None, None], from_cache, from_new)


def dilated_attn_sample(q, cache, kv_new, window, dil):
    Lc = cache.shape[1]
    T = q.shape[1]
    n = window // dil
    idx = (Lc + jnp.arange(T))[:, None] - dil * jnp.arange(n + 1)[None, :]
    valid = idx >= 0
    rows = gather_rows(cache, kv_new, jnp.maximum(idx, 0))
    k, v = rows[:, :, :, 0], rows[:, :, :, 1]
    s = jnp.einsum('bthd,btjhd->bthj', q, k, preferred_element_type=jnp.float32) / math.sqrt(q.shape[-1])
    s = jnp.where(valid[None, :, None, :], s, -jnp.inf)
    m = jnp.max(s, axis=-1)
    p = jnp.exp(s - m[..., None])
    l = jnp.sum(p, axis=-1)
    o = jnp.einsum('bthj,btjhd->bthd', p, v.astype(jnp.float32)) / l[..., None]
    return o, m + jnp.log(l)


def attn_out(outs, gate, w_out, dt):
    o = jnp.stack([oo for oo, _ in outs], axis=0)
    lse = jnp.stack([ll for _, ll in outs], axis=0)
    wts = jax.nn.softmax(lse, axis=0)
    comb = jnp.sum(wts[..., None] * o, axis=0)
    B, T = comb.shape[:2]
    y = comb.reshape(B, T, ATTN_WIDTH) * jax.nn.silu(gate.astype(jnp.float32))
    return y.astype(dt) @ w_out


def setup_inputs(seed: int = 0) -> dict:
    key = jax.random.key(seed)
    ks = jax.random.split(key, 20)

    def nrm(k, shape, scale):
        return jax.random.normal(k, shape, jnp.float32) * scale

    cache_shapes = [(DEC_BATCH, min(w, PAST_LEN), 2, HEADS_PER_GROUP, HEAD_DIM) for w, _ in ATTN_PATTERNS]
    return {
        'x_prompt': nrm(ks[0], (BATCH, SEQ, D_MODEL), 1.0),
        'x_sample': nrm(ks[1], (DEC_BATCH, DEC_SEQ, D_MODEL), 1.0),
        'state_pool': nrm(ks[2], (N_A_LAYERS, DEC_BATCH, POOL_STATE, POOL_WIDTH), 1.0),
        'cache_kv_w128': nrm(ks[3], cache_shapes[0], 1.0),
        'cache_kv_w512': nrm(ks[4], cache_shapes[1], 1.0),
        'cache_kv_w2048': nrm(ks[5], cache_shapes[2], 1.0),
        'g_a': 1.0 + nrm(ks[6], (N_A_LAYERS, D_MODEL), 0.05),
        'w_a_in': nrm(ks[7], (N_A_LAYERS, D_MODEL, 2 * POOL_WIDTH), D_MODEL ** -0.5),
        'w_a_group': nrm(ks[8], (N_A_LAYERS, N_POOL_GROUPS, POOL_GROUP_WIDTH, POOL_GROUP_WIDTH), POOL_GROUP_WIDTH ** -0.5),
        'a_scale': 1.0 + nrm(ks[9], (N_A_LAYERS, POOL_WIDTH), 0.1),
        'w_a_out': nrm(ks[10], (N_A_LAYERS, POOL_WIDTH, D_MODEL), POOL_WIDTH ** -0.5),
        'g_kv': 1.0 + nrm(ks[11], (D_MODEL,), 0.05),
        'w_kv': nrm(ks[12], (D_MODEL, N_GROUPS * 2 * ATTN_WIDTH), D_MODEL ** -0.5),
        'g_b': 1.0 + nrm(ks[13], (N_B_LAYERS, D_MODEL), 0.05),
        'w_b_in': nrm(ks[14], (N_B_LAYERS, D_MODEL, N_GROUPS * ATTN_WIDTH + ATTN_WIDTH), D_MODEL ** -0.5),
        'w_b_out': nrm(ks[15], (N_B_LAYERS, ATTN_WIDTH, D_MODEL), ATTN_WIDTH ** -0.5),
        'g_final': 1.0 + nrm(ks[16], (D_MODEL,), 0.05),
    }


def reference(x_prompt, x_sample, state_pool, cache_kv_w128, cache_kv_w512, cache_kv_w2048,
              g_a, w_a_in, w_a_group, a_scale, w_a_out, g_kv, w_kv, g_b, w_b_in, w_b_out, g_final):
    dt = x_prompt.dtype
    S = x_prompt.shape[1]
    T = x_sample.shape[1]
    pos_p = jnp.arange(S, dtype=jnp.int32)
    pos_s = PAST_LEN + jnp.arange(T, dtype=jnp.int32)
    pos_buf = PAST_LEN - POOL_STATE + jnp.arange(POOL_STATE + T, dtype=jnp.int32)
    caches = (cache_kv_w128, cache_kv_w512, cache_kv_w2048)
    keep = [min(w, S) for w, _ in ATTN_PATTERNS]

    h_p, h_s = x_prompt, x_sample
    pool_new_p, pool_new_s = [], []
    kv_p, kv_s = None, None
    for layer in range(DEPTH):
        if layer < N_A_LAYERS:
            i = layer
            u_p, gate_p = pool_in(h_p, g_a[i], w_a_in[i])
            u_s, gate_s = pool_in(h_s, g_a[i], w_a_in[i])
            u_full = jnp.concatenate([state_pool[i].astype(u_s.dtype), u_s], axis=1)
            r_p = pool_residual(u_p, pos_p)
            r_s = pool_residual(u_full, pos_buf)[:, POOL_STATE:]
            h_p = h_p + pool_out(r_p, gate_p, w_a_group[i], a_scale[i], w_a_out[i], dt)
            h_s = h_s + pool_out(r_s, gate_s, w_a_group[i], a_scale[i], w_a_out[i], dt)
            pool_new_p.append(u_p[:, -POOL_STATE:])
            pool_new_s.append(u_full[:, -POOL_STATE:])
        else:
            if layer == N_A_LAYERS:
                kv_p = shared_kv(h_p, g_kv, w_kv, pos_p)
                kv_s = shared_kv(h_s, g_kv, w_kv, pos_s)
            j = layer - N_A_LAYERS
            q_p, gate_p = attn_in(h_p, g_b[j], w_b_in[j], pos_p)
            q_s, gate_s = attn_in(h_s, g_b[j], w_b_in[j], pos_s)
            outs_p = [dilated_attn_prompt(q_p[:, :, g], kv_p[:, :, g, 0], kv_p[:, :, g, 1], w, d)
                      for g, (w, d) in enumerate(ATTN_PATTERNS)]
            outs_s = [dilated_attn_sample(q_s[:, :, g], caches[g], kv_s[:, :, g], w, d)
                      for g, (w, d) in enumerate(ATTN_PATTERNS)]
            h_p = h_p + attn_out(outs_p, gate_p, w_b_out[j], dt)
            h_s = h_s + attn_out(outs_s, gate_s, w_b_out[j], dt)

    y_prompt = rmsnorm(h_p, g_final)
    y_sample = rmsnorm(h_s, g_final)
    pool_prompt = jnp.stack(pool_new_p, axis=0)
    pool_sample = jnp.stack(pool_new_s, axis=0)
    return (y_prompt, y_sample, pool_prompt, pool_sample,
            kv_p[:, S - keep[0]:, 0], kv_s[:, :, 0],
            kv_p[:, S - keep[1]:, 1], kv_s[:, :, 1],
            kv_p[:, S - keep[2]:, 2], kv_s[:, :, 2])
```

```python
import math
from contextlib import ExitStack

import numpy as np
import concourse.bass as bass
import concourse.mybir as mybir
from concourse.bass_utils import run_bass_kernel_spmd

F32 = mybir.dt.float32
BF16 = mybir.dt.bfloat16
ALU = mybir.AluOpType
AF = mybir.ActivationFunctionType
AX = mybir.AxisListType

NCORES = 8
D = 1024
SEQ = 8192
OWN = 2048
HALO = 2048
AH = 512
EXT = AH + HALO + OWN
PAST = 2048
EPS = 1e-6
DILS = (1, 4, 16)
ROPE_THETA = 500000.0
NTA = 256
NT = 512
import os
STOP = int(os.environ.get('KSTOP', '99'))


class _Rec:
    def __init__(self):
        self.calls = []

    def __getattr__(self, name):
        def f(*a, **kw):
            self.calls.append((name, a, kw))
            return self
        return f


class Sched:
    ENGS = ("pe", "act", "dve", "pool", "sp")

    def __init__(self, nc, n_dma_sems=32):
        self.nc = nc
        self.streams = {e: [] for e in self.ENGS}
        self.cnt = {e: 0 for e in self.ENGS}
        self.dma_cnt = {e: 0 for e in self.ENGS}
        self.n_dma_sems = n_dma_sems
        self.last_w = {}
        self.readers = {}
        self.known = {e: {} for e in self.ENGS}
        self.sem_max = {}
        self.final_tokens = []

    def _need(self, eng, tok, waits):
        sem, v = tok
        if self.known[eng].get(sem, 0) >= v:
            return
        self.known[eng][sem] = v
        waits[sem] = max(waits.get(sem, 0), v)

    def op(self, eng, fn, reads=(), writes=(), dma=False, final=False):
        waits = {}
        for k in reads:
            w = self.last_w.get(k)
            if w is not None:
                self._need(eng, w, waits)
        for k in writes:
            w = self.last_w.get(k)
            if w is not None:
                self._need(eng, w, waits)
            for sem, v in self.readers.get(k, {}).items():
                self._need(eng, (sem, v), waits)
        if dma:
            j = self.dma_cnt[eng]
            self.dma_cnt[eng] += 1
            sem = "d_%s_%d" % (eng, j % self.n_dma_sems)
            use = j // self.n_dma_sems
            if use > 0:
                self._need(eng, (sem, 16 * use), waits)
            tok = (sem, 16 * (use + 1))
            inc = 16
        else:
            self.cnt[eng] += 1
            sem = "c_%s" % eng
            tok = (sem, self.cnt[eng])
            inc = 1
        self.sem_max[sem] = tok[1]
        rec = _Rec()
        fn(rec)
        assert rec.calls
        self.streams[eng].append((sorted(waits.items()), rec.calls, sem, inc))
        for k in writes:
            self.last_w[k] = tok
            self.readers[k] = {}
        for k in reads:
            d = self.readers.setdefault(k, {})
            d[tok[0]] = max(d.get(tok[0], 0), tok[1])
        if final:
            self.final_tokens.append(tok)
        return tok

    def barrier(self):
        for eng in self.ENGS:
            waits = {}
            for sem, v in self.sem_max.items():
                self._need(eng, (sem, v), waits)
            if waits:
                self.streams[eng].append((sorted(waits.items()), None, None, 0))

    def emit(self, stack):
        nc = self.nc
        sems = {}
        for name in sorted(self.sem_max):
            sems[name] = stack.enter_context(nc.semaphore(name))
        block = stack.enter_context(nc.Block())
        fin = dict(self.sem_max)
        streams = self.streams

        def run(engname, e, extra_final=False):
            for waits, fn, sem, inc in streams[engname]:
                for s, v in waits:
                    e.wait_ge(sems[s], v)
                if fn is None:
                    continue
                ins = None
                for name, a, kw in fn:
                    ins = getattr(e, name)(*a, **kw)
                ins.then_inc(sems[sem], inc)
            if extra_final:
                for s, v in sorted(fin.items()):
                    e.wait_ge(sems[s], v)

        @block.sync
        def _(e):
            run("sp", e, extra_final=True)

        @block.tensor
        def _(e):
            run("pe", e)

        @block.scalar
        def _(e):
            run("act", e)

        @block.vector
        def _(e):
            run("dve", e)

        @block.gpsimd
        def _(e):
            run("pool", e)


class Arena:
    def __init__(self, t, cols):
        self.t = t
        self.cols = cols
        self.off = 0

    def f32(self, n):
        a = self.off
        self.off += n
        self.hw = max(getattr(self, "hw", 0), self.off)
        assert self.off <= self.cols, ("arena overflow", self.off, self.cols)
        return self.t[:, a:a + n]

    def bf16(self, n):
        m = (n + 1) // 2
        a = self.off
        self.off += m
        self.hw = max(getattr(self, "hw", 0), self.off)
        assert self.off <= self.cols, ("arena overflow", self.off, self.cols)
        return self.t[:, a:a + m].bitcast(BF16)[:, 0:n]

    def mark(self):
        return self.off

    def reset(self, m):
        self.off = m


def build_program():
    nc = bass.Bass("TRN2", target_bir_lowering=False)

    def din(name, shape):
        return nc.dram_tensor(name, list(shape), F32, kind="ExternalInput").ap()

    def dout(name, shape):
        return nc.dram_tensor(name, list(shape), F32, kind="ExternalOutput").ap()

    def dscr(name, shape, dt):
        return nc.dram_tensor(name, list(shape), dt, kind="Internal").ap()

    xT_h = din("xT", [8, 128, EXT])
    invc_h = din("invc", [4, EXT])
    cos_h = din("cosT", [128, EXT])
    sin_h = din("sinT", [128, EXT])
    xs_h = din("xs", [16, D])
    st_h = din("st", [2, 16, 15, D])
    ck_h = [din("ck%d" % g, [16, 128, 2, D]) for g in range(3)]
    wain_h = din("wain", [2, D, 2048])
    wagrp_h = din("wagrp", [2, 4, 256, 256])
    waout_h = din("waout", [2, D, D])
    wk_h = din("wk", [D, 3072])
    wv_h = din("wv", [D, 3072])
    wbq_h = din("wbq", [2, D, 3072])
    wbg_h = din("wbg", [2, D, D])
    wbout_h = din("wbout", [2, D, D])
    vcol_h = din("vcol", [128, 64])
    vrow_h = din("vrow", [8, D])
    csrow_h = din("csrow", [2, 128])
    mask_h = din("mask", [128, 384])
    ident_h = din("ident", [128, 128])
    sel_h = din("sel", [16, 16 * 128])
    eb_h = din("eb", [128, 16 * 16])

    yT_o = dout("yT", [8, 128, OWN])
    ys_o = dout("ys", [16, D])
    poolp_o = dout("poolp", [2, 8, 128, 15])
    pools_o = dout("pools", [2, 16, 15, D])
    kT_o = [dout("kT0", [8, 128, 512]), dout("kT1", [8, 128, 512]), dout("kT2", [8, 128, OWN])]
    v_o = [dout("v0", [512, D]), dout("v1", [512, D]), dout("v2", [OWN, D])]
    ks_o = dout("ks", [16, 3, D])
    vs_o = dout("vs", [16, 3, D])

    H = dscr("H", [8, 128, EXT], F32)
    KTall = dscr("KTall", [8, 128, 8832], BF16)
    VS = [dscr("VS%d" % g, [8, HALO + OWN, 128], BF16) for g in range(3)]
    QTall = dscr("QTall", [8, 128, 3 * OWN], BF16)

    S = Sched(nc)
    with ExitStack() as st:
        ACOLS = int(os.environ.get("ACOLS", "52900"))
        arena_t = st.enter_context(nc.sbuf_tensor("arena", [128, ACOLS], F32))
        A = Arena(arena_t, ACOLS)
        PS = [st.enter_context(nc.psum_tensor("ps%d" % i, [128, 512], F32)) for i in range(8)]
        PSK = ["ps%d" % i for i in range(8)]

        vcol = A.f32(64)
        ident = A.f32(128)
        identb = A.bf16(128)
        ones_bf = A.bf16(128)
        mask = A.bf16(384)
        hs = A.f32(D)
        csrow = A.f32(256)
        eb = A.bf16(256)
        s_g = A.f32(D)
        P0 = A.mark()

        S.op("sp", lambda e: e.dma_start(out=vcol, in_=vcol_h), writes=["vcol"], dma=True)
        S.op("sp", lambda e: e.dma_start(out=ident, in_=ident_h), writes=["ident"], dma=True)
        S.op("pool", lambda e: e.dma_start(out=identb, in_=ident_h), writes=["identb"], dma=True)
        S.op("pool", lambda e: e.dma_start(out=mask, in_=mask_h), writes=["mask"], dma=True)
        S.op("pool", lambda e: e.dma_start(out=eb, in_=eb_h), writes=["eb"], dma=True)
        S.op("pool", lambda e: e.memset(ones_bf, 1.0), writes=["ones"])
        S.op("sp", lambda e: e.dma_start(out=hs[0:16, :], in_=xs_h), writes=["hs"], dma=True)
        S.op("sp", lambda e: e.dma_start(out=csrow[0:16, :].rearrange("p (v n) -> p v n", v=2),
                                         in_=csrow_h.unsqueeze(0).broadcast_to([16, 2, 128])),
             writes=["csrow"], dma=True)

        VGA, VAS, VGKV, VGB, VGF = 0, 2, 4, 5, 7

        def gcol(vi, k):
            return vcol[:, vi * 8 + k: vi * 8 + k + 1]

        def load_grow(vi):
            S.op("sp", lambda e: e.dma_start(out=s_g[0:16, :], in_=vrow_h[vi:vi + 1, :].broadcast_to([16, D])),
                 writes=["s_g"], dma=True)
            return s_g[0:16, :]

        psrot = [0]

        def next_ps(lo=0, hi=7):
            i = lo + psrot[0] % (hi - lo)
            psrot[0] += 1
            return PS[i], PSK[i]

        def mm_group(out_ap, pairs, reads, wkey):
            def fn(e):
                ins = None
                n = len(pairs)
                for i, (l, r) in enumerate(pairs):
                    ins = e.matmul(out_ap, l, r, start=(i == 0), stop=(i == n - 1))
                return ins
            S.op("pe", fn, reads=reads, writes=[wkey])

        def rms_fm(h3, n, vi, hsq3, hn3, rstd, pfx, spfx=None, hnpfx=None):
            sp_ = pfx if spfx is None else spfx
            hp_ = pfx if hnpfx is None else hnpfx
            for k in range(8):
                S.op("act", lambda e, k=k: e.activation(out=hsq3[:, k, :], in_=h3[:, k, :], func=AF.Square),
                     reads=[pfx + "h%d" % k], writes=[sp_ + "hsq%d" % k])
            mm_group(PS[7][:, 0:n], [(ones_bf, hsq3[:, k, :]) for k in range(8)],
                     ["ones"] + [sp_ + "hsq%d" % k for k in range(8)], PSK[7])
            S.op("dve", lambda e: e.tensor_scalar(out=rstd, in0=PS[7][:, 0:n], scalar1=1.0 / D, scalar2=EPS,
                                                  op0=ALU.mult, op1=ALU.add), reads=[PSK[7]], writes=[sp_ + "rstd"])
            S.op("act", lambda e: e.activation(out=rstd, in_=rstd, func=AF.Sqrt),
                 reads=[sp_ + "rstd"], writes=[sp_ + "rstd"])
            S.op("dve", lambda e: e.reciprocal(out=rstd, in_=rstd), reads=[sp_ + "rstd"], writes=[sp_ + "rstd"])
            for k in range(8):
                S.op("dve", lambda e, k=k: e.scalar_tensor_tensor(out=hn3[:, k, :], in0=h3[:, k, :], scalar=gcol(vi, k),
                                                                  in1=rstd, op0=ALU.mult, op1=ALU.mult),
                     reads=[pfx + "h%d" % k, sp_ + "rstd", "vcol"], writes=[hp_ + "hn%d" % k])

        def s_rms(src, vi, dst, tmp, col):
            gr = load_grow(vi)
            S.op("dve", lambda e: e.tensor_tensor(out=tmp, in0=src, in1=src, op=ALU.mult), reads=["hs"], writes=["s_tmp"])
            S.op("dve", lambda e: e.tensor_reduce(out=col, in_=tmp, axis=AX.X, op=ALU.add), reads=["s_tmp"], writes=["s_col"])
            S.op("dve", lambda e: e.tensor_scalar(out=col, in0=col, scalar1=1.0 / D, scalar2=EPS, op0=ALU.mult, op1=ALU.add),
                 reads=["s_col"], writes=["s_col"])
            S.op("act", lambda e: e.activation(out=col, in_=col, func=AF.Sqrt), reads=["s_col"], writes=["s_col"])
            S.op("dve", lambda e: e.reciprocal(out=col, in_=col), reads=["s_col"], writes=["s_col"])
            S.op("dve", lambda e: e.scalar_tensor_tensor(out=dst, in0=src, scalar=col, in1=gr, op0=ALU.mult, op1=ALU.mult),
                 reads=["hs", "s_col", "s_g"], writes=["s_hn"])

        def s_T(src, skey, dstT, dkey):
            def fn(e):
                ins = None
                for k in range(8):
                    ins = e.transpose(PS[7][:, k * 16:(k + 1) * 16], src[:, k * 128:(k + 1) * 128], ident[0:16, 0:16])
                return ins
            S.op("pe", fn, reads=[skey, "ident"], writes=[PSK[7]])
            S.op("act", lambda e: e.activation(out=dstT, in_=PS[7][:, 0:128], func=AF.Copy), reads=[PSK[7]], writes=[dkey])

        def s_proj(hT, hkey, w3, wkey, col0, ncols, bank, bkey):
            hT3 = hT.rearrange("p (k t) -> p k t", k=8)
            mm_group(PS[bank][0:16, 0:ncols], [(hT3[:, k, :], w3[:, k, col0:col0 + ncols]) for k in range(8)],
                     [hkey, wkey], bkey)

        KSs = dscr("KSs", [16, 3 * D], F32)
        VSs = dscr("VSs", [16, 3 * D], F32)
        QSs = dscr("QSs", [16, 3 * D], F32)
        SGs = dscr("SGs", [16, D], F32)
        hs16 = hs[0:16, :]
        rot = {"kbf": 0, "k32": 0, "v": 0}

        for l in range(2):
            A.reset(P0)
            wain = A.bf16(8 * 2048).rearrange("p (k n) -> p k n", k=8)
            wagrp = A.bf16(4 * 2 * 256).rearrange("p (g k n) -> p g k n", g=4, k=2)
            waout = A.bf16(8 * D).rearrange("p (k n) -> p k n", k=8)
            S.op("pool", lambda e, l=l: e.dma_start(out=wain, in_=wain_h[l].rearrange("(k p) n -> p k n", p=128)),
                 writes=["wain"], dma=True)
            S.op("pool", lambda e, l=l: e.dma_start(out=wagrp, in_=wagrp_h[l].rearrange("g (k p) n -> p g k n", p=128)),
                 writes=["wagrp"], dma=True)
            S.op("pool", lambda e, l=l: e.dma_start(out=waout, in_=waout_h[l].rearrange("(k p) n -> p k n", p=128)),
                 writes=["waout"], dma=True)
            PAW = A.mark()
            h3s = [A.f32(8 * NT).rearrange("p (k n) -> p k n", k=8) for _ in range(2)]
            hsq3 = A.bf16(8 * NT).rearrange("p (k n) -> p k n", k=8)
            hn3 = A.bf16(8 * NT).rearrange("p (k n) -> p k n", k=8)
            rstd = A.f32(NT)
            UW = 15 + NT
            uexts = [A.f32(8 * UW).rearrange("p (k n) -> p k n", k=8) for _ in range(2)]
            carry = [A.f32(8 * 15).rearrange("p (k n) -> p k n", k=8) for _ in range(3)]
            tA = A.f32(UW)
            tB = A.f32(UW)
            sg3s = [A.bf16(8 * NT).rearrange("p (k n) -> p k n", k=8) for _ in range(2)]
            r3 = A.bf16(8 * NT).rearrange("p (k n) -> p k n", k=8)
            y3 = A.bf16(8 * NT).rearrange("p (k n) -> p k n", k=8)
            invc3s = [A.f32(4 * NT).rearrange("p (g n) -> p g n", g=4) for _ in range(2)]
            ntilesA = EXT // NT
            src_h = xT_h if l == 0 else H

            def a_load_sq(t, l=l, src_h=src_h):
                p_ = t % 2
                pfx = "A%d" % p_
                c0 = t * NT
                h3 = h3s[p_]
                S.op("sp", lambda e: e.dma_start(out=h3, in_=src_h[:, :, c0:c0 + NT].rearrange("k p n -> p k n")),
                     reads=["H"], writes=[pfx + "h%d" % k for k in range(8)], dma=True)
                S.op("sp", lambda e: e.dma_start(out=invc3s[p_], in_=invc_h[:, c0:c0 + NT].unsqueeze(0).broadcast_to([128, 4, NT])),
                     writes=[pfx + "invc"], dma=True)
                for k in range(8):
                    S.op("act", lambda e, k=k: e.activation(out=hsq3[:, k, :], in_=h3[:, k, :], func=AF.Square),
                         reads=[pfx + "h%d" % k], writes=["Ahsq%d" % k])
                mm_group(PS[7][:, 0:NT], [(ones_bf, hsq3[:, k, :]) for k in range(8)],
                         ["ones"] + ["Ahsq%d" % k for k in range(8)], PSK[7])

            def a_rstd(t):
                S.op("dve", lambda e: e.tensor_scalar(out=rstd, in0=PS[7][:, 0:NT], scalar1=1.0 / D, scalar2=EPS,
                                                      op0=ALU.mult, op1=ALU.add), reads=[PSK[7]], writes=["Arstd"])
                S.op("act", lambda e: e.activation(out=rstd, in_=rstd, func=AF.Sqrt), reads=["Arstd"], writes=["Arstd"])
                S.op("dve", lambda e: e.reciprocal(out=rstd, in_=rstd), reads=["Arstd"], writes=["Arstd"])

            def a_hn(t, l=l):
                p_ = t % 2
                pfx = "A%d" % p_
                h3 = h3s[p_]
                for k in range(8):
                    S.op("dve", lambda e, k=k: e.scalar_tensor_tensor(out=hn3[:, k, :], in0=h3[:, k, :], scalar=gcol(VGA + l, k),
                                                                      in1=rstd, op0=ALU.mult, op1=ALU.mult),
                         reads=[pfx + "h%d" % k, "Arstd", "vcol"], writes=["Ahn%d" % k])

            def a_inproj(t):
                p_ = t % 2
                pfx = "A%d" % p_
                uext, sg3 = uexts[p_], sg3s[p_]
                for oc in range(16):
                    ps, pk = next_ps()
                    mm_group(ps[:, 0:NT], [(wain[:, k, oc * 128:(oc + 1) * 128], hn3[:, k, :]) for k in range(8)],
                             ["wain"] + ["Ahn%d" % k for k in range(8)], pk)
                    if oc < 8:
                        S.op("act", lambda e, ps=ps, oc=oc: e.activation(out=uext[:, oc, 15:15 + NT], in_=ps[:, 0:NT], func=AF.Copy),
                             reads=[pk], writes=[pfx + "ue%d" % oc])
                    else:
                        S.op("act", lambda e, ps=ps, oc=oc: e.activation(out=sg3[:, oc - 8, :], in_=ps[:, 0:NT], func=AF.Silu),
                             reads=[pk], writes=[pfx + "sg%d" % (oc - 8)])
                S.op("pool", lambda e: e.tensor_copy(out=carry[t % 3], in_=uext[:, :, NT:NT + 15]),
                     reads=[pfx + "ue%d" % k for k in range(8)], writes=["carry%d" % (t % 3)])

            def a_stageB(t, part, l=l):
                p_ = t % 2
                pfx = "A%d" % p_
                c0 = t * NT
                h3, uext, sg3, invc3 = h3s[p_], uexts[p_], sg3s[p_], invc3s[p_]
                uks = [pfx + "ue%d" % k for k in range(8)]
                if part == "pool":
                    a_b_pool(t, p_, pfx, c0, h3, uext, sg3, invc3, uks)
                elif part == "z":
                    a_b_z(t, p_, pfx, c0, h3, uext, sg3, invc3, uks)
                else:
                    a_b_out(t, p_, pfx, c0, h3, uext, sg3, invc3, uks)

            def a_b_pool(t, p_, pfx, c0, h3, uext, sg3, invc3, uks, l=l):
                if t == 0:
                    S.op("pool", lambda e: e.memset(uext[:, :, 0:15], 0.0), reads=uks, writes=uks)
                else:
                    S.op("pool", lambda e: e.tensor_copy(out=uext[:, :, 0:15], in_=carry[(t - 1) % 3]),
                         reads=["carry%d" % ((t - 1) % 3)] + uks, writes=uks)
                for oc in range(8):
                    g = oc // 2
                    uk = pfx + "ue%d" % oc
                    src = uext[:, oc, :]
                    eng = "dve"
                    ta, tb = (tA, "tA"), (tB, "tB")
                    cur, curk = src, uk
                    lo = 0
                    for lev in range(g + 1):
                        sh = 1 << lev
                        dst, dk = ta if lev % 2 == 0 else tb
                        nlo = lo + sh
                        S.op(eng, lambda e, dst=dst, cur=cur, nlo=nlo, sh=sh: e.tensor_tensor(
                            out=dst[:, nlo:UW], in0=cur[:, nlo:UW], in1=cur[:, nlo - sh:UW - sh], op=ALU.add),
                            reads=[curk], writes=[dk])
                        cur, curk, lo = dst, dk, nlo
                    oth, othk = tb if curk == ta[1] else ta
                    if c0 <= AH + HALO < c0 + NT:
                        S.op(eng, lambda e, cur=cur, oth=oth, g=g: e.tensor_tensor(out=oth[:, 15:UW], in0=cur[:, 15:UW], in1=invc3[:, g, :], op=ALU.mult),
                             reads=[curk, pfx + "invc"], writes=[othk])
                        S.op(eng, lambda e, oth=oth, src=src, oc=oc: e.tensor_tensor(out=r3[:, oc, :], in0=oth[:, 15:UW], in1=src[:, 15:UW], op=ALU.subtract),
                             reads=[othk, uk], writes=["Ar%d" % oc])
                    else:
                        S.op(eng, lambda e, cur=cur, src=src, oc=oc, g=g: e.scalar_tensor_tensor(
                            out=r3[:, oc, :], in0=cur[:, 15:UW], scalar=1.0 / (2 ** (g + 1)), in1=src[:, 15:UW], op0=ALU.mult, op1=ALU.subtract),
                            reads=[curk, uk], writes=["Ar%d" % oc])
                    if t == ntilesA - 1:
                        S.op("sp", lambda e, oc=oc: e.dma_start(out=poolp_o[l, oc], in_=uext[:, oc, NT:NT + 15]),
                             reads=[uk], dma=True, final=True)

            def a_b_z(t, p_, pfx, c0, h3, uext, sg3, invc3, uks, l=l):
                for oc in range(8):
                    g = oc // 2
                    ps, pk = next_ps()
                    mm_group(ps[:, 0:NT], [(wagrp[:, g, kk, (oc % 2) * 128:(oc % 2 + 1) * 128], r3[:, 2 * g + kk, :]) for kk in range(2)],
                             ["wagrp", "Ar%d" % (2 * g), "Ar%d" % (2 * g + 1)], pk)
                    S.op("dve", lambda e, ps=ps, oc=oc: e.scalar_tensor_tensor(out=y3[:, oc, :], in0=ps[:, 0:NT], scalar=gcol(VAS + l, oc),
                                                                          in1=sg3[:, oc, :], op0=ALU.mult, op1=ALU.mult),
                         reads=[pk, pfx + "sg%d" % oc, "vcol"], writes=["Ay%d" % oc])

            def a_b_out(t, p_, pfx, c0, h3, uext, sg3, invc3, uks, l=l):
                for dc in range(8):
                    ps, pk = next_ps()
                    mm_group(ps[:, 0:NT], [(waout[:, k, dc * 128:(dc + 1) * 128], y3[:, k, :]) for k in range(8)],
                             ["waout"] + ["Ay%d" % k for k in range(8)], pk)
                    S.op("dve", lambda e, ps=ps, dc=dc: e.tensor_tensor(out=h3[:, dc, :], in0=h3[:, dc, :], in1=ps[:, 0:NT], op=ALU.add),
                         reads=[pk, pfx + "h%d" % dc], writes=[pfx + "h%d" % dc])
                S.op("sp", lambda e: e.dma_start(out=H[:, :, c0:c0 + NT].rearrange("k p n -> p k n"), in_=h3),
                     reads=[pfx + "h%d" % k for k in range(8)], writes=["Hst"], dma=True)

            a_load_sq(0)
            a_rstd(0)
            a_hn(0)
            a_inproj(0)
            for t in range(ntilesA):
                nxt = t + 1 < ntilesA
                if nxt:
                    a_load_sq(t + 1)
                a_stageB(t, "pool")
                if nxt:
                    a_rstd(t + 1)
                a_stageB(t, "z")
                if nxt:
                    a_hn(t + 1)
                a_stageB(t, "out")
                if nxt:
                    a_inproj(t + 1)

            S.barrier()
            A.reset(PAW)
            s_tmp = A.f32(D)[0:16, :]
            s_col = A.f32(1)[0:16, :]
            s_hn = A.f32(D)[0:16, :]
            s_hT = A.bf16(128)
            s_us = A.f32(D)[0:16, :]
            s_sg = A.f32(D)[0:16, :]
            s_r = A.f32(D)[0:16, :]
            s_y = A.f32(D)[0:16, :]
            s_rT = A.bf16(128)
            s_yT = A.bf16(128)
            s_st = A.f32(15 * 256)[0:16, :]
            s_rms(hs16, VGA + l, s_hn, s_tmp, s_col)
            s_T(s_hn, "s_hn", s_hT, "s_hT")
            for q in range(4):
                s_proj(s_hT, "s_hT", wain, "wain", q * 512, 512, q, PSK[q])
            for q in range(2):
                S.op("act", lambda e, q=q: e.activation(out=s_us[:, q * 512:(q + 1) * 512], in_=PS[q][0:16, :], func=AF.Copy),
                     reads=[PSK[q]], writes=["s_us"])
                S.op("act", lambda e, q=q: e.activation(out=s_sg[:, q * 512:(q + 1) * 512], in_=PS[2 + q][0:16, :], func=AF.Silu),
                     reads=[PSK[2 + q]], writes=["s_sg"])
            S.op("sp", lambda e, l=l: e.dma_start(out=pools_o[l, :, 14, :], in_=s_us), reads=["s_us"], dma=True, final=True)
            S.op("sp", lambda e, l=l: e.dma_start(out=pools_o[l, :, 0:14, :], in_=st_h[l, :, 1:15, :]), dma=True, final=True)
            for g in range(4):
                w = 2 ** (g + 1)
                sl = slice(g * 256, (g + 1) * 256)
                stv = s_st[:, 0:(w - 1) * 256].rearrange("p (j c) -> p j c", c=256)
                S.op("sp", lambda e, stv=stv, w=w, sl=sl, l=l: e.dma_start(out=stv, in_=st_h[l, :, 15 - (w - 1):15, sl]),
                     writes=["s_st"], dma=True)
                S.op("dve", lambda e, stv=stv, sl=sl: e.tensor_reduce(out=s_r[:, sl], in_=stv.rearrange("p j c -> p c j"), axis=AX.X, op=ALU.add),
                     reads=["s_st"], writes=["s_r"])
                S.op("dve", lambda e, sl=sl: e.tensor_tensor(out=s_r[:, sl], in0=s_r[:, sl], in1=s_us[:, sl], op=ALU.add),
                     reads=["s_r", "s_us"], writes=["s_r"])
                S.op("dve", lambda e, sl=sl, w=w: e.scalar_tensor_tensor(out=s_r[:, sl], in0=s_r[:, sl], scalar=1.0 / w, in1=s_us[:, sl],
                                                                    op0=ALU.mult, op1=ALU.subtract),
                     reads=["s_r", "s_us"], writes=["s_r"])
            s_T(s_r, "s_r", s_rT, "s_rT")
            rT3 = s_rT.rearrange("p (k t) -> p k t", k=8)
            for g in range(4):
                bank = g // 2
                mm_group(PS[bank][0:16, (g % 2) * 256:(g % 2 + 1) * 256],
                         [(rT3[:, 2 * g + kk, :], wagrp[:, g, kk, :]) for kk in range(2)], ["s_rT", "wagrp"], PSK[bank])
            asr = load_grow(VAS + l)
            for q in range(2):
                sl = slice(q * 512, (q + 1) * 512)
                S.op("dve", lambda e, q=q, sl=sl: e.tensor_tensor(out=s_y[:, sl], in0=PS[q][0:16, :], in1=asr[:, sl], op=ALU.mult),
                     reads=[PSK[q], "s_g"], writes=["s_y"])
            S.op("dve", lambda e: e.tensor_tensor(out=s_y, in0=s_y, in1=s_sg, op=ALU.mult), reads=["s_y", "s_sg"], writes=["s_y"])
            s_T(s_y, "s_y", s_yT, "s_yT")
            for q in range(2):
                s_proj(s_yT, "s_yT", waout, "waout", q * 512, 512, q, PSK[q])
                sl = slice(q * 512, (q + 1) * 512)
                S.op("dve", lambda e, q=q, sl=sl: e.tensor_tensor(out=hs16[:, sl], in0=hs16[:, sl], in1=PS[q][0:16, :], op=ALU.add),
                     reads=[PSK[q], "hs"], writes=["hs"])
            S.barrier()

        if STOP == 1:
            S.emit(st)
            return nc
        A.reset(P0)

        wk = A.bf16(8 * 3072).rearrange("p (k n) -> p k n", k=8)
        wv = A.bf16(8 * 3072).rearrange("p (k n) -> p k n", k=8)
        S.op("pool", lambda e: e.dma_start(out=wk, in_=wk_h.rearrange("(k p) n -> p k n", p=128)), writes=["wk"], dma=True)
        S.op("pool", lambda e: e.dma_start(out=wv, in_=wv_h.rearrange("(k p) n -> p k n", p=128)), writes=["wv"], dma=True)
        PKW = A.mark()
        h3s = [A.f32(8 * NT).rearrange("p (k n) -> p k n", k=8) for _ in range(2)]
        hsq3 = A.bf16(8 * NT).rearrange("p (k n) -> p k n", k=8)
        hn3s = [A.bf16(8 * NT).rearrange("p (k n) -> p k n", k=8) for _ in range(2)]
        rstd = A.f32(NT)
        cosTs = [A.f32(NT) for _ in range(2)]
        sinTs = [A.f32(NT) for _ in range(2)]
        h3, hn3, cosT, sinT = h3s[0], hn3s[0], cosTs[0], sinTs[0]
        kx = [A.f32(NT), A.f32(NT)]
        rt = [A.f32(NT) for _ in range(4)]
        kr = [A.f32(NT), A.f32(NT)]
        kbf = [A.bf16(NT) for _ in range(5)]
        k32 = [A.f32(NT) for _ in range(1)]
        vbf = [A.bf16(D) for _ in range(3)]
        v32 = [A.f32(D) for _ in range(1)]

        def rope_fm(x0, x1, k0, k1, pfx, n=NT):
            S.op("dve", lambda e: e.tensor_tensor(out=rt[0][:, 0:n], in0=x0, in1=cosT[:, 0:n], op=ALU.mult), reads=[k0, pfx + "cs"], writes=["rt0"])
            S.op("pool", lambda e: e.tensor_tensor(out=rt[1][:, 0:n], in0=x1, in1=sinT[:, 0:n], op=ALU.mult), reads=[k1, pfx + "cs"], writes=["rt1"])
            S.op("dve", lambda e: e.tensor_tensor(out=rt[2][:, 0:n], in0=x0, in1=sinT[:, 0:n], op=ALU.mult), reads=[k0, pfx + "cs"], writes=["rt2"])
            S.op("pool", lambda e: e.tensor_tensor(out=rt[3][:, 0:n], in0=x1, in1=cosT[:, 0:n], op=ALU.mult), reads=[k1, pfx + "cs"], writes=["rt3"])
            S.op("dve", lambda e: e.tensor_tensor(out=kr[0][:, 0:n], in0=rt[0][:, 0:n], in1=rt[1][:, 0:n], op=ALU.subtract), reads=["rt0", "rt1"], writes=["kr0"])
            S.op("pool", lambda e: e.tensor_tensor(out=kr[1][:, 0:n], in0=rt[2][:, 0:n], in1=rt[3][:, 0:n], op=ALU.add), reads=["rt2", "rt3"], writes=["kr1"])

        def fm_qk_chunks(w3, wkey, g, pfx, scale, dst_fn, need32, out32_fn):
            for c in range(8):
                ps, pk = next_ps()
                col = g * 1024 + c * 128
                mm_group(ps[:, 0:NT], [(w3[:, k, col:col + 128], hn3[:, k, :]) for k in range(8)],
                         [wkey] + [pfx + "hn%d" % k for k in range(8)], pk)
                if c < 2:
                    S.op("act", lambda e, ps=ps, c=c: e.activation(out=kx[c], in_=ps[:, 0:NT], func=AF.Copy), reads=[pk], writes=["kx%d" % c])
                    if c == 1:
                        rope_fm(kx[0], kx[1], "kx0", "kx1", pfx)
                        for cc in range(2):
                            i = rot["kbf"] % len(kbf)
                            rot["kbf"] += 1
                            S.op("act", lambda e, i=i, cc=cc: e.activation(out=kbf[i], in_=kr[cc], func=AF.Copy, scale=scale),
                                 reads=["kr%d" % cc], writes=["kbf%d" % i])
                            dst_fn(cc, kbf[i], "kbf%d" % i)
                            if need32:
                                out32_fn(cc, kr[cc], "kr%d" % cc)
                else:
                    i = rot["kbf"] % len(kbf)
                    rot["kbf"] += 1
                    S.op("act", lambda e, ps=ps, i=i: e.activation(out=kbf[i], in_=ps[:, 0:NT], func=AF.Copy, scale=scale),
                         reads=[pk], writes=["kbf%d" % i])
                    dst_fn(c, kbf[i], "kbf%d" % i)
                    if need32:
                        j = rot["k32"] % len(k32)
                        rot["k32"] += 1
                        S.op("dve", lambda e, ps=ps, j=j: e.tensor_copy(out=k32[j], in_=ps[:, 0:NT]), reads=[pk], writes=["k32%d" % j])
                        out32_fn(c, k32[j], "k32%d" % j)

        def kv_load_h(t):
            pb_ = t % 2
            pfx = "K%d" % pb_
            e0 = AH + t * NT
            S.op("sp", lambda e: e.dma_start(out=h3s[pb_], in_=H[:, :, e0:e0 + NT].rearrange("k p n -> p k n")),
                 reads=["H"], writes=[pfx + "h%d" % k for k in range(8)], dma=True)

        def kv_stageA(t):
            pb_ = t % 2
            pfx = "K%d" % pb_
            e0 = AH + t * NT
            S.op("sp", lambda e: e.dma_start(out=cosTs[pb_], in_=cos_h[:, e0:e0 + NT]), writes=[pfx + "cs"], dma=True)
            S.op("sp", lambda e: e.dma_start(out=sinTs[pb_], in_=sin_h[:, e0:e0 + NT]), writes=[pfx + "cs"], dma=True)
            rms_fm(h3s[pb_], NT, VGKV, hsq3, hn3s[pb_], rstd, pfx, "K")

        kv_load_h(0)
        kv_load_h(1)
        kv_stageA(0)
        for t in range(8):
            e0 = AH + t * NT
            k0 = t * NT
            own = t >= 4
            groups = [2] if t < 3 else [0, 1, 2]
            if t + 2 < 8:
                kv_load_h(t + 2)
            if t + 1 < 8:
                kv_stageA(t + 1)
            KP = "K%d" % (t % 2)
            h3, hn3, cosT, sinT = h3s[t % 2], hn3s[t % 2], cosTs[t % 2], sinTs[t % 2]
            for g in groups:
                need32 = own and (g == 2 or t == 7)
                ocol = ((t - 4) * NT if g == 2 else 0) if need32 else 0

                def dst_fn(c, tile, key, g=g, k0=k0):
                    tok0 = HALO - 128 * DILS[g]
                    lo = max(k0, tok0)
                    if lo >= k0 + NT:
                        return
                    koff = (0, 2176, 2176 + 2560)[g]
                    S.op("sp", lambda e: e.dma_start(out=KTall[c, :, koff + lo - tok0:koff + k0 + NT - tok0], in_=tile[:, lo - k0:NT]),
                         reads=[key], writes=["KT"], dma=True)

                def out32_fn(c, tile, key, g=g, ocol=ocol):
                    S.op("sp", lambda e: e.dma_start(out=kT_o[g][c, :, ocol:ocol + NT], in_=tile), reads=[key], dma=True, final=True)

                fm_qk_chunks(wk, "wk", g, KP, 1.0, dst_fn, need32, out32_fn)
                for sub in range(4):
                    j = rot["v"] % len(vbf)
                    rot["v"] += 1
                    for half in range(2):
                        ps, pk = next_ps()
                        col = g * 1024 + half * 512
                        mm_group(ps[:, 0:512], [(hn3[:, k, sub * 128:(sub + 1) * 128], wv[:, k, col:col + 512]) for k in range(8)],
                                 ["wv"] + [KP + "hn%d" % k for k in range(8)], pk)
                        S.op("act", lambda e, ps=ps, j=j, half=half: e.activation(out=vbf[j][:, half * 512:(half + 1) * 512], in_=ps[:, 0:512], func=AF.Copy),
                             reads=[pk], writes=["vbf%d" % j])
                        if need32:
                            S.op("dve", lambda e, ps=ps, half=half, j=j: e.tensor_copy(out=v32[0][:, half * 512:(half + 1) * 512], in_=ps[:, 0:512]),
                                 reads=[pk], writes=["v32"])
                    r0 = k0 + sub * 128
                    S.op("sp", lambda e, j=j, g=g, r0=r0: e.dma_start(out=VS[g][:, r0:r0 + 128, :].rearrange("hp n d -> n hp d"),
                                                                 in_=vbf[j].rearrange("p (hp d) -> p hp d", hp=8)),
                         reads=["vbf%d" % j], writes=["VS"], dma=True)
                    if need32:
                        orow = ocol + sub * 128
                        S.op("sp", lambda e, g=g, orow=orow, j=j: e.dma_start(out=v_o[g][orow:orow + 128, :], in_=v32[0]),
                             reads=["v32"], dma=True, final=True)

        S.barrier()
        A.reset(PKW)
        s_tmp = A.f32(D)[0:16, :]
        s_col = A.f32(1)[0:16, :]
        s_hn = A.f32(D)[0:16, :]
        s_hT = A.bf16(128)
        s_kd = A.f32(D)[0:16, :]
        s_t = [A.f32(128)[0:16, :] for _ in range(4)]
        ksn3 = A.f32(3 * D)[0:16, :].rearrange("p (g n) -> p g n", g=3)
        vsn3 = A.f32(3 * D)[0:16, :].rearrange("p (g n) -> p g n", g=3)
        cs16 = csrow[0:16, 0:128]
        sn16 = csrow[0:16, 128:256]

        def s_rope_perm(src, skey, dst, dkey, scale):
            x0 = src[:, 0:128]
            x1 = src[:, 128:256]
            S.op("dve", lambda e: e.tensor_tensor(out=s_t[0], in0=x0, in1=cs16, op=ALU.mult), reads=[skey, "csrow"], writes=["s_t0"])
            S.op("dve", lambda e: e.tensor_tensor(out=s_t[1], in0=x1, in1=sn16, op=ALU.mult), reads=[skey, "csrow"], writes=["s_t1"])
            S.op("dve", lambda e: e.tensor_tensor(out=s_t[2], in0=x0, in1=sn16, op=ALU.mult), reads=[skey, "csrow"], writes=["s_t2"])
            S.op("dve", lambda e: e.tensor_tensor(out=s_t[3], in0=x1, in1=cs16, op=ALU.mult), reads=[skey, "csrow"], writes=["s_t3"])
            S.op("dve", lambda e: e.tensor_tensor(out=x0, in0=s_t[0], in1=s_t[1], op=ALU.subtract), reads=["s_t0", "s_t1"], writes=[skey])
            S.op("dve", lambda e: e.tensor_tensor(out=x1, in0=s_t[2], in1=s_t[3], op=ALU.add), reads=["s_t2", "s_t3"], writes=[skey])
            S.op("act", lambda e: e.activation(out=dst.rearrange("p (h c i) -> p h c i", h=16, c=8),
                                               in_=src.rearrange("p (c h i) -> p h c i", c=8, h=16), func=AF.Copy, scale=scale),
                 reads=[skey], writes=[dkey])

        s_rms(hs16, VGKV, s_hn, s_tmp, s_col)
        s_T(s_hn, "s_hn", s_hT, "s_hT")
        for g in range(3):
            for q in range(2):
                s_proj(s_hT, "s_hT", wk, "wk", g * 1024 + q * 512, 512, q, PSK[q])
                s_proj(s_hT, "s_hT", wv, "wv", g * 1024 + q * 512, 512, 2 + q, PSK[2 + q])
                sl = slice(q * 512, (q + 1) * 512)
                S.op("act", lambda e, q=q, sl=sl: e.activation(out=s_kd[:, sl], in_=PS[q][0:16, :], func=AF.Copy), reads=[PSK[q]], writes=["s_kd"])
                S.op("act", lambda e, q=q, sl=sl, g=g: e.activation(out=vsn3[:, g, sl], in_=PS[2 + q][0:16, :], func=AF.Copy), reads=[PSK[2 + q]], writes=["vsn"])
            s_rope_perm(s_kd, "s_kd", ksn3[:, g, :], "ksn", 1.0)
        S.op("sp", lambda e: e.dma_start(out=ks_o.rearrange("p g n -> p (g n)"), in_=ksn3.rearrange("p g n -> p (g n)")), reads=["ksn"], dma=True, final=True)
        S.op("sp", lambda e: e.dma_start(out=vs_o.rearrange("p g n -> p (g n)"), in_=vsn3.rearrange("p g n -> p (g n)")), reads=["vsn"], dma=True, final=True)
        S.op("sp", lambda e: e.dma_start(out=KSs, in_=ksn3.rearrange("p g n -> p (g n)")), reads=["ksn"], writes=["KSs"], dma=True)
        S.op("sp", lambda e: e.dma_start(out=VSs, in_=vsn3.rearrange("p g n -> p (g n)")), reads=["vsn"], writes=["VSs"], dma=True)

        S.barrier()
        if STOP == 2:
            S.emit(st)
            return nc
        A.reset(P0)

        GEOM = []
        for g, d in enumerate(DILS):
            GEOM.append((d, OWN // (128 * d), 128 * d))
        KW = [GEOM[g][2] + OWN for g in range(3)]
        KOFF = [0, KW[0], KW[0] + KW[1]]
        NBLK = [(GEOM[g][1] + 1) * GEOM[g][0] for g in range(3)]
        BOFF = [0, NBLK[0], NBLK[0] + NBLK[1]]

        sgB = A.bf16(8 * OWN).rearrange("p (k n) -> p k n", k=8)
        PSG = A.mark()
        yTB = A.bf16(8 * OWN).rearrange("p (k n) -> p k n", k=8)
        PB1 = A.mark()
        for j in range(2):
            A.reset(PSG)
            wbq = A.bf16(8 * 3072).rearrange("p (k n) -> p k n", k=8)
            wbg = A.bf16(8 * D).rearrange("p (k n) -> p k n", k=8)
            S.op("pool", lambda e, j=j: e.dma_start(out=wbq, in_=wbq_h[j].rearrange("(k p) n -> p k n", p=128)), writes=["wbq"], dma=True)
            S.op("pool", lambda e, j=j: e.dma_start(out=wbg, in_=wbg_h[j].rearrange("(k p) n -> p k n", p=128)), writes=["wbg"], dma=True)
            PBW = A.mark()
            h3s = [A.f32(8 * NT).rearrange("p (k n) -> p k n", k=8) for _ in range(2)]
            hsq3 = A.bf16(8 * NT).rearrange("p (k n) -> p k n", k=8)
            hn3s = [A.bf16(8 * NT).rearrange("p (k n) -> p k n", k=8) for _ in range(2)]
            rstd = A.f32(NT)
            cosTs = [A.f32(NT) for _ in range(2)]
            sinTs = [A.f32(NT) for _ in range(2)]
            kx = [A.f32(NT), A.f32(NT)]
            rt = [A.f32(NT) for _ in range(4)]
            kr = [A.f32(NT), A.f32(NT)]
            kbf = [A.bf16(NT) for _ in range(5)]

            def b1_load_h(t):
                pb_ = t % 2
                pfx = "B%d" % pb_
                e0 = AH + HALO + t * NT
                S.op("sp", lambda e: e.dma_start(out=h3s[pb_], in_=H[:, :, e0:e0 + NT].rearrange("k p n -> p k n")),
                     reads=["H"], writes=[pfx + "h%d" % k for k in range(8)], dma=True)

            def b1_stageA(t, j=j):
                pb_ = t % 2
                pfx = "B%d" % pb_
                e0 = AH + HALO + t * NT
                S.op("sp", lambda e: e.dma_start(out=cosTs[pb_], in_=cos_h[:, e0:e0 + NT]), writes=[pfx + "cs"], dma=True)
                S.op("sp", lambda e: e.dma_start(out=sinTs[pb_], in_=sin_h[:, e0:e0 + NT]), writes=[pfx + "cs"], dma=True)
                rms_fm(h3s[pb_], NT, VGB + j, hsq3, hn3s[pb_], rstd, pfx, "B")

            b1_load_h(0)
            b1_load_h(1)
            b1_stageA(0)
            for t in range(4):
                q0 = t * NT
                if t + 2 < 4:
                    b1_load_h(t + 2)
                if t + 1 < 4:
                    b1_stageA(t + 1)
                BP = "B%d" % (t % 2)
                h3, hn3, cosT, sinT = h3s[t % 2], hn3s[t % 2], cosTs[t % 2], sinTs[t % 2]
                for g in range(3):
                    def dst_fn(c, tile, key, g=g, q0=q0):
                        S.op("sp", lambda e: e.dma_start(out=QTall[c, :, g * OWN + q0:g * OWN + q0 + NT], in_=tile), reads=[key], writes=["QT"], dma=True)
                    fm_qk_chunks(wbq, "wbq", g, BP, 0.125, dst_fn, False, None)
                for oc in range(8):
                    ps, pk = next_ps()
                    mm_group(ps[:, 0:NT], [(wbg[:, k, oc * 128:(oc + 1) * 128], hn3[:, k, :]) for k in range(8)],
                             ["wbg"] + [BP + "hn%d" % k for k in range(8)], pk)
                    S.op("act", lambda e, ps=ps, oc=oc, q0=q0: e.activation(out=sgB[:, oc, q0:q0 + NT], in_=ps[:, 0:NT], func=AF.Silu),
                         reads=[pk], writes=["sgB%d" % oc])
            S.barrier()
            A.reset(PBW)
            s_tmp = A.f32(D)[0:16, :]
            s_col = A.f32(1)[0:16, :]
            s_hn = A.f32(D)[0:16, :]
            s_hT = A.bf16(128)
            s_kd = A.f32(D)[0:16, :]
            s_t = [A.f32(128)[0:16, :] for _ in range(4)]
            qsn3 = A.f32(3 * D)[0:16, :].rearrange("p (g n) -> p g n", g=3)
            sgs = A.f32(D)[0:16, :]
            s_rms(hs16, VGB + j, s_hn, s_tmp, s_col)
            s_T(s_hn, "s_hn", s_hT, "s_hT")
            for g in range(3):
                for q in range(2):
                    s_proj(s_hT, "s_hT", wbq, "wbq", g * 1024 + q * 512, 512, q, PSK[q])
                    sl = slice(q * 512, (q + 1) * 512)
                    S.op("act", lambda e, q=q, sl=sl: e.activation(out=s_kd[:, sl], in_=PS[q][0:16, :], func=AF.Copy), reads=[PSK[q]], writes=["s_kd"])
                s_rope_perm(s_kd, "s_kd", qsn3[:, g, :], "qsn", 0.125)
            for q in range(2):
                s_proj(s_hT, "s_hT", wbg, "wbg", q * 512, 512, 2 + q, PSK[2 + q])
                sl = slice(q * 512, (q + 1) * 512)
                S.op("act", lambda e, q=q, sl=sl: e.activation(out=sgs[:, sl], in_=PS[2 + q][0:16, :], func=AF.Silu), reads=[PSK[2 + q]], writes=["sgs"])
            S.op("sp", lambda e: e.dma_start(out=QSs, in_=qsn3.rearrange("p g n -> p (g n)")), reads=["qsn"], writes=["QSs"], dma=True)
            S.op("sp", lambda e: e.dma_start(out=SGs, in_=sgs), reads=["sgs"], writes=["SGs"], dma=True)

            S.barrier()
            if STOP == 3:
                S.emit(st)
                return nc
            A.reset(PB1)

            KTh = [A.bf16(8832) for _ in range(2)]
            Vu = [A.bf16(NBLK[g_] * 256).rearrange("p (b x) -> p b x", x=256) for g_ in range(3)]
            QTh = [A.bf16(3 * OWN) for _ in range(2)]
            NPT = 6
            PT = [A.bf16(256) for _ in range(NPT)]
            ACC = [A.f32(OWN), A.f32(OWN)]
            REC = A.f32(OWN)
            for i in range(3):
                S.op("pool", lambda e, i=i: e.memset(Vu[i], 1.0), writes=["Vu%d" % i])
            SB = [0, 1, 2, 3]
            OB = [4, 5, 6, 7]
            LAGP = 2
            tiles = []
            orot = [0]
            ucount = 0
            for hp in range(8):
                for g in range(3):
                    d, NB, halo = GEOM[g]
                    ub = ucount % 3
                    ucount += 1
                    first_of_unit = True
                    for r in range(d):
                        okeys = {}
                        for kb in range(-1, NB):
                            for hh in range(2):
                                qbs = [qb for qb in (kb, kb + 1) if 0 <= qb < NB]
                                kc0 = halo + kb * 128 * d + r
                                q0c = qbs[0] * 128 * d + r
                                nq = 128 * len(qbs)
                                if kb == -1:
                                    mk = mask[:, 256:384]
                                elif kb == NB - 1:
                                    mk = mask[:, 0:128]
                                else:
                                    mk = mask[:, 0:256]
                                pvs = []
                                for qi, qb in enumerate(qbs):
                                    if (hh, qb) not in okeys:
                                        okeys[(hh, qb)] = OB[orot[0] % len(OB)]
                                        orot[0] += 1
                                    a0 = qb * 128 * d + r
                                    pvs.append((qi, okeys[(hh, qb)], kb == qb - 1, kb == qb, slice(a0, a0 + 127 * d + 1, d)))
                                tiles.append(dict(hp=hp, g=g, hh=hh, ub=ub, load=first_of_unit, load_hp=(first_of_unit and g == 0),
                                                  kcols=slice(kc0, kc0 + 127 * d + 1, d),
                                                  qcols=slice(q0c, q0c + (nq - 1) * d + 1, d), nq=nq, mk=mk,
                                                  blk=r * (NB + 1) + (kb + 1), pvs=pvs,
                                                  last_of_head=(g == 2 and r == d - 1 and kb == NB - 1)))
                                first_of_unit = False

            def emit_loads(T):
                hp, g, ub = T["hp"], T["g"], T["ub"]
                hb = hp % 2
                d, NB, halo = GEOM[g]
                tok0 = HALO - halo
                if T["load_hp"]:
                    for hh in range(2):
                        for c in range(8):
                            p0 = hh * 64 + c * 8
                            hrow = (2 * hp + hh) * 8
                            S.op("sp", lambda e, p0=p0, hrow=hrow, c=c: e.dma_start(
                                out=KTh[hb][p0:p0 + 8, :], in_=KTall[c, hrow:hrow + 8, :]),
                                reads=["KT"], writes=["KTh%d" % hb], dma=True)
                            S.op("sp", lambda e, p0=p0, hrow=hrow, c=c: e.dma_start(
                                out=QTh[hb][p0:p0 + 8, :], in_=QTall[c, hrow:hrow + 8, :]),
                                reads=["QT"], writes=["QTh%d" % hb], dma=True)
                for r in range(d):
                    b0 = r * (NB + 1)
                    S.op("sp", lambda e, r=r, b0=b0: e.dma_start(
                        out=Vu[ub][:, b0:b0 + NB + 1, 64:192],
                        in_=VS[g][hp, tok0 + r:HALO + OWN:d, :].rearrange("(kb a) x -> a kb x", a=128)),
                        reads=["VS"], writes=["Vu%d" % ub], dma=True)

            def emit_front(i, T):
                if T["load"]:
                    emit_loads(T)
                ub, pb, nq = T["ub"], 64 * T["hh"], T["nq"]
                si = SB[i % len(SB)]
                pt, ptk = PT[i % NPT], "PT%d" % (i % NPT)
                hb = T["hp"] % 2
                g_ = T["g"]
                kc, qc = T["kcols"], T["qcols"]
                kcols = slice(KOFF[g_] + kc.start, KOFF[g_] + kc.stop, kc.step)
                qcols = slice(g_ * OWN + qc.start, g_ * OWN + qc.stop, qc.step)
                S.op("pe", lambda e: e.matmul(PS[si][:, 0:nq], KTh[hb][pb:pb + 64, kcols], QTh[hb][pb:pb + 64, qcols],
                                              start=True, stop=True),
                     reads=["KTh%d" % hb, "QTh%d" % hb], writes=[PSK[si]])
                S.op("act", lambda e: e.activation(out=pt[:, 0:nq], in_=PS[si][:, 0:nq], func=AF.Exp), reads=[PSK[si]], writes=[ptk])
                S.op("dve" if i % 4 == 3 else "pool", lambda e: e.tensor_tensor(out=pt[:, 0:nq], in0=pt[:, 0:nq], in1=T["mk"], op=ALU.mult),
                     reads=[ptk, "mask"], writes=[ptk])

            def emit_back(i, T):
                ub, hh, g, hp = T["ub"], T["hh"], T["g"], T["hp"]
                pt, ptk = PT[i % NPT], "PT%d" % (i % NPT)
                lhsT = Vu[ub][:, T["blk"], 128 * hh:128 * hh + 128]
                acc, acck = ACC[hh], "ACC%d" % hh
                for (qi, oi, is_first, is_last, acols) in T["pvs"]:
                    S.op("pe", lambda e, qi=qi, oi=oi, is_first=is_first, is_last=is_last: e.matmul(
                        PS[oi][:, 0:128], lhsT, pt[:, qi * 128:(qi + 1) * 128], start=is_first, stop=is_last),
                        reads=[ptk, "Vu%d" % ub], writes=[PSK[oi]])
                    if is_last:
                        if g == 0:
                            S.op("act", lambda e, oi=oi, acols=acols: e.activation(out=acc[:, acols], in_=PS[oi][:, 0:128], func=AF.Copy),
                                 reads=[PSK[oi]], writes=[acck])
                        else:
                            S.op("dve", lambda e, oi=oi, acols=acols: e.tensor_tensor(out=acc[:, acols], in0=acc[:, acols], in1=PS[oi][:, 0:128], op=ALU.add),
                                 reads=[PSK[oi], acck], writes=[acck])
                if T["last_of_head"]:
                    nb = 64 - 64 * hh
                    db = 64 - nb
                    S.op("dve", lambda e: e.reciprocal(out=REC[nb:nb + 64, :], in_=acc[db:db + 64, :]), reads=[acck], writes=["REC"])
                    S.op("dve", lambda e: e.tensor_tensor(out=REC[nb:nb + 64, :], in0=REC[nb:nb + 64, :], in1=acc[nb:nb + 64, :], op=ALU.mult),
                         reads=[acck, "REC"], writes=["REC"])
                    S.op("pool", lambda e: e.tensor_tensor(out=yTB[nb:nb + 64, hp, :], in0=REC[nb:nb + 64, :], in1=sgB[nb:nb + 64, hp, :], op=ALU.mult),
                         reads=["REC", "sgB%d" % hp], writes=["yTB%d" % hp])

            ntl = len(tiles)
            assert ntl % 2 == 0
            npairs = ntl // 2
            for jp in range(npairs + LAGP):
                if jp < npairs:
                    emit_front(2 * jp, tiles[2 * jp])
                    emit_front(2 * jp + 1, tiles[2 * jp + 1])
                if jp >= LAGP:
                    emit_back(2 * (jp - LAGP), tiles[2 * (jp - LAGP)])
                    emit_back(2 * (jp - LAGP) + 1, tiles[2 * (jp - LAGP) + 1])

            S.barrier()
            if STOP == 4:
                S.emit(st)
                return nc
            A.reset(PB1)

            s_yT = A.bf16(128)
            PB3 = A.mark()
            Kc = [A.f32(D) for _ in range(3)]
            Vc = [A.f32(D) for _ in range(3)]
            prod = A.f32(D)
            sc = A.f32(16)
            p32 = A.f32(16)
            PVt = [A.bf16(1040) for _ in range(2)]
            PVn = A.bf16(1040)[0:16, :]
            prodn = A.f32(D)[0:16, :]
            scn = A.f32(16)[0:16, :]
            pn32 = A.f32(16)[0:16, :]
            recs = A.f32(16)[0:16, :]
            comb = A.f32(D)[0:16, :]
            sel = A.bf16(16 * 128)
            qb16 = A.bf16(3 * D)[0:16, :].rearrange("p (g n) -> p g n", g=3)
            qsn3 = A.f32(3 * D)[0:16, :].rearrange("p (g n) -> p g n", g=3)
            ksn3 = A.f32(3 * D)[0:16, :].rearrange("p (g n) -> p g n", g=3)
            vsn3 = A.f32(3 * D)[0:16, :].rearrange("p (g n) -> p g n", g=3)
            sgs = A.f32(D)[0:16, :]
            S.op("pool", lambda e: e.dma_start(out=sel[0:16, :], in_=sel_h), writes=["sel"], dma=True)
            S.op("sp", lambda e: e.dma_start(out=qsn3.rearrange("p g n -> p (g n)"), in_=QSs), reads=["QSs"], writes=["qsn"], dma=True)
            S.op("sp", lambda e: e.dma_start(out=ksn3.rearrange("p g n -> p (g n)"), in_=KSs), reads=["KSs"], writes=["ksn"], dma=True)
            S.op("sp", lambda e: e.dma_start(out=vsn3.rearrange("p g n -> p (g n)"), in_=VSs), reads=["VSs"], writes=["vsn"], dma=True)
            S.op("sp", lambda e: e.dma_start(out=sgs, in_=SGs), reads=["SGs"], writes=["sgs"], dma=True)
            S.op("act", lambda e: e.activation(out=qb16, in_=qsn3, func=AF.Copy), reads=["qsn"], writes=["qb16"])
            eb3 = eb.rearrange("p (b m) -> p b m", b=16)
            sel3 = sel[0:16, :].rearrange("p (b m) -> p b m", b=16)
            ACCB = (PS[5], PS[6], PS[7])
            ACCK = (PSK[5], PSK[6], PSK[7])

            def acc_mm(lhsT, rhs, rkeys, first, last):
                def fn(e):
                    e.matmul(ACCB[0][0:16, 0:512], lhsT, rhs[:, 0:512], start=first, stop=last)
                    e.matmul(ACCB[1][0:16, 0:512], lhsT, rhs[:, 512:1024], start=first, stop=last)
                    return e.matmul(ACCB[2][0:16, 0:16], lhsT, rhs[:, 1024:1040], start=first, stop=last)
                S.op("pe", fn, reads=rkeys, writes=list(ACCK))

            nsteps = 3 + 48
            step = 0
            for g in range(3):
                S.op("dve", lambda e, g=g: e.tensor_tensor(out=prodn, in0=qsn3[:, g, :], in1=ksn3[:, g, :], op=ALU.mult),
                     reads=["qsn", "ksn"], writes=["prodn"])
                S.op("dve", lambda e: e.tensor_reduce(out=scn, in_=prodn.rearrange("p (h d) -> p h d", h=16), axis=AX.X, op=ALU.add),
                     reads=["prodn"], writes=["scn"])
                S.op("act", lambda e: e.activation(out=pn32, in_=scn, func=AF.Exp), reads=["scn"], writes=["pn32"])
                S.op("act", lambda e: e.activation(out=PVn[:, 1024:1040], in_=pn32, func=AF.Copy), reads=["pn32"], writes=["PVn"])
                S.op("dve", lambda e, g=g: e.tensor_tensor(out=PVn[:, 0:1024].rearrange("p (h d) -> p h d", h=16),
                                                       in0=vsn3[:, g, :].rearrange("p (h d) -> p h d", h=16),
                                                       in1=pn32.unsqueeze(2).broadcast_to([16, 16, 64]), op=ALU.mult),
                     reads=["vsn", "pn32", "PVn"], writes=["PVn"])
                step += 1
                acc_mm(identb[0:16, 0:16], PVn, ["PVn", "identb"], step == 1, False)
            sc2 = [sc, A.f32(16)]
            p322 = [p32, A.f32(16)]
            its = [(b, g) for b in range(16) for g in range(3)]

            def s_front(n_, b, g):
                i = n_ % 2
                ik = n_ % 3
                S.op("sp", lambda e: e.dma_start(out=Kc[ik], in_=ck_h[g][b, :, 0, :]), writes=["Kc%d" % ik], dma=True)
                S.op("sp", lambda e: e.dma_start(out=Vc[ik], in_=ck_h[g][b, :, 1, :]), writes=["Vc%d" % ik], dma=True)
                for q in range(2):
                    bk = 2 * i + q
                    S.op("pe", lambda e, q=q, bk=bk: e.matmul(PS[bk][:, 0:512], sel3[:, b, :], qb16[:, g, q * 512:(q + 1) * 512], start=True, stop=True),
                         reads=["sel", "qb16"], writes=[PSK[bk]])
                    S.op("dve", lambda e, q=q, bk=bk: e.tensor_tensor(out=prod[:, q * 512:(q + 1) * 512], in0=Kc[ik][:, q * 512:(q + 1) * 512], in1=PS[bk][:, 0:512], op=ALU.mult),
                         reads=[PSK[bk], "Kc%d" % ik], writes=["prod"])
                S.op("dve", lambda e: e.tensor_reduce(out=sc2[i], in_=prod.rearrange("p (h d) -> p h d", h=16), axis=AX.X, op=ALU.add),
                     reads=["prod"], writes=["sc%d" % i])
                S.op("act", lambda e: e.activation(out=p322[i], in_=sc2[i], func=AF.Exp), reads=["sc%d" % i], writes=["p32%d" % i])
                S.op("act", lambda e: e.activation(out=PVt[i][:, 1024:1040], in_=p322[i], func=AF.Copy), reads=["p32%d" % i], writes=["PVt%d" % i])
                S.op("pool", lambda e: e.tensor_tensor(out=PVt[i][:, 0:1024].rearrange("p (h d) -> p h d", h=16),
                                                       in0=Vc[ik].rearrange("p (h d) -> p h d", h=16),
                                                       in1=p322[i].unsqueeze(2).broadcast_to([128, 16, 64]), op=ALU.mult),
                     reads=["Vc%d" % ik, "p32%d" % i, "PVt%d" % i], writes=["PVt%d" % i])

            def s_back(n_, b, g):
                i = n_ % 2
                acc_mm(eb3[:, b, :], PVt[i], ["PVt%d" % i, "eb"], False, n_ == len(its) - 1)

            for n_ in range(len(its) + 1):
                if n_ < len(its):
                    s_front(n_, *its[n_])
                if n_ >= 1:
                    s_back(n_ - 1, *its[n_ - 1])
            S.op("dve", lambda e: e.reciprocal(out=recs, in_=ACCB[2][0:16, 0:16]), reads=[ACCK[2]], writes=["recs"])
            for q in range(2):
                for ho in range(2):
                    hi = 1 - ho
                    S.op("dve", lambda e, q=q, ho=ho, hi=hi: e.tensor_tensor(
                        out=comb[:, q * 512:(q + 1) * 512].rearrange("p (hp hh d) -> p hp hh d", hp=4, hh=2)[:, :, ho, :],
                        in0=ACCB[q][0:16, 0:512].rearrange("p (hp hh d) -> p hp hh d", hp=4, hh=2)[:, :, hi, :],
                        in1=recs[:, q * 8:(q + 1) * 8].rearrange("p (hp hh) -> p hp hh", hh=2)[:, :, hi].unsqueeze(2).broadcast_to([16, 4, 64]),
                        op=ALU.mult),
                        reads=[ACCK[q], "recs"], writes=["comb"])
            S.op("dve", lambda e: e.tensor_tensor(out=comb, in0=comb, in1=sgs, op=ALU.mult), reads=["comb", "sgs"], writes=["comb"])
            s_T(comb, "comb", s_yT, "s_yT")
            S.barrier()
            if STOP == 5:
                S.emit(st)
                return nc
            A.reset(PB3)

            wbo = A.bf16(8 * D).rearrange("p (k n) -> p k n", k=8)
            S.op("pool", lambda e, j=j: e.dma_start(out=wbo, in_=wbout_h[j].rearrange("(k p) n -> p k n", p=128)), writes=["wbo"], dma=True)
            h3s = [A.f32(8 * NT).rearrange("p (k n) -> p k n", k=8) for _ in range(2)]
            hsq3 = A.bf16(8 * NT).rearrange("p (k n) -> p k n", k=8)
            rstd = A.f32(NT)
            yo3 = A.f32(8 * NT).rearrange("p (k n) -> p k n", k=8)

            def b3_load(t):
                e0 = AH + HALO + t * NT
                S.op("sp", lambda e: e.dma_start(out=h3s[t % 2], in_=H[:, :, e0:e0 + NT].rearrange("k p n -> p k n")),
                     reads=["H"], writes=["O%dh%d" % (t % 2, k) for k in range(8)], dma=True)

            for t in range(4):
                b3_load(t)
                if t == 1:
                    break
            for t in range(4):
                e0 = AH + HALO + t * NT
                q0 = t * NT
                h3 = h3s[t % 2]
                OP = "O%d" % (t % 2)
                for dc in range(8):
                    ps, pk = next_ps()
                    mm_group(ps[:, 0:NT], [(wbo[:, hp, dc * 128:(dc + 1) * 128], yTB[:, hp, q0:q0 + NT]) for hp in range(8)],
                             ["wbo"] + ["yTB%d" % hp for hp in range(8)], pk)
                    S.op("dve", lambda e, ps=ps, dc=dc: e.tensor_tensor(out=h3[:, dc, :], in0=h3[:, dc, :], in1=ps[:, 0:NT], op=ALU.add),
                         reads=[pk, OP + "h%d" % dc], writes=[OP + "h%d" % dc])
                if j == 0:
                    S.op("sp", lambda e, e0=e0: e.dma_start(out=H[:, :, e0:e0 + NT].rearrange("k p n -> p k n"), in_=h3),
                         reads=[OP + "h%d" % k for k in range(8)], writes=["Hout"], dma=True)
                else:
                    for k in range(8):
                        S.op("act", lambda e, k=k: e.activation(out=hsq3[:, k, :], in_=h3[:, k, :], func=AF.Square),
                             reads=[OP + "h%d" % k], writes=["Ohsq%d" % k])
                    mm_group(PS[7][:, 0:NT], [(ones_bf, hsq3[:, k, :]) for k in range(8)], ["ones"] + ["Ohsq%d" % k for k in range(8)], PSK[7])
                    S.op("dve", lambda e: e.tensor_scalar(out=rstd, in0=PS[7][:, 0:NT], scalar1=1.0 / D, scalar2=EPS, op0=ALU.mult, op1=ALU.add),
                         reads=[PSK[7]], writes=["Orstd"])
                    S.op("act", lambda e: e.activation(out=rstd, in_=rstd, func=AF.Sqrt), reads=["Orstd"], writes=["Orstd"])
                    S.op("dve", lambda e: e.reciprocal(out=rstd, in_=rstd), reads=["Orstd"], writes=["Orstd"])
                    for k in range(8):
                        S.op("dve", lambda e, k=k: e.scalar_tensor_tensor(out=yo3[:, k, :], in0=h3[:, k, :], scalar=gcol(VGF, k), in1=rstd,
                                                                      op0=ALU.mult, op1=ALU.mult),
                             reads=[OP + "h%d" % k, "Orstd", "vcol"], writes=["yo%d" % k])
                    S.op("sp", lambda e, q0=q0: e.dma_start(out=yT_o[:, :, q0:q0 + NT].rearrange("k p n -> p k n"), in_=yo3),
                         reads=["yo%d" % k for k in range(8)], dma=True, final=True)
                if t + 2 < 4:
                    b3_load(t + 2)
            for q in range(2):
                s_proj(s_yT, "s_yT", wbo, "wbo", q * 512, 512, q, PSK[q])
                sl = slice(q * 512, (q + 1) * 512)
                S.op("dve", lambda e, q=q, sl=sl: e.tensor_tensor(out=hs16[:, sl], in0=hs16[:, sl], in1=PS[q][0:16, :], op=ALU.add),
                     reads=[PSK[q], "hs"], writes=["hs"])
            if j == 1:
                s_tmp = A.f32(D)[0:16, :]
                s_col = A.f32(1)[0:16, :]
                s_hn = A.f32(D)[0:16, :]
                s_rms(hs16, VGF, s_hn, s_tmp, s_col)
                S.op("sp", lambda e: e.dma_start(out=ys_o, in_=s_hn), reads=["s_hn"], dma=True, final=True)

            S.barrier()

        if os.environ.get('KDEBUG'):
            print('arena high-water', A.hw)
        S.emit(st)
    return nc


def _dim_major_perm():
    idx = np.empty(1024, dtype=np.int64)
    for c in range(8):
        for h in range(16):
            for i in range(8):
                idx[c * 128 + h * 8 + i] = h * 64 + c * 8 + i
    return idx


def kernel(x_prompt, x_sample, state_pool, cache_kv_w128, cache_kv_w512, cache_kv_w2048,
           g_a, w_a_in, w_a_group, a_scale, w_a_out, g_kv, w_kv, g_b, w_b_in, w_b_out, g_final):
    f32 = np.float32
    x_prompt = np.asarray(x_prompt, f32)
    caches = [np.asarray(cache_kv_w128, f32), np.asarray(cache_kv_w512, f32), np.asarray(cache_kv_w2048, f32)]
    perm = _dim_major_perm()
    w_kv = np.asarray(w_kv, f32)
    w_b_in = np.asarray(w_b_in, f32)
    wk = np.concatenate([w_kv[:, (2 * g) * 1024:(2 * g + 1) * 1024][:, perm] for g in range(3)], axis=1)
    wv = np.concatenate([w_kv[:, (2 * g + 1) * 1024:(2 * g + 2) * 1024] for g in range(3)], axis=1)
    wbq = np.stack([np.concatenate([w_b_in[j][:, g * 1024:(g + 1) * 1024][:, perm] for g in range(3)], axis=1) for j in range(2)])
    sw = np.arange(1024).reshape(8, 2, 64)[:, ::-1, :].reshape(1024)
    wbg = np.ascontiguousarray(w_b_in[:, :, 3072:4096][:, :, sw])
    vecs = [g_a[0], g_a[1], a_scale[0], a_scale[1], g_kv, g_b[0], g_b[1], g_final]
    vecs = [np.asarray(v, f32) for v in vecs]
    vrow = np.stack(vecs)
    vcol = np.concatenate([v.reshape(8, 128).T for v in vecs], axis=1)
    inv = (1.0 / (ROPE_THETA ** (np.arange(0, 16, 2, dtype=np.float32) / 16.0))).astype(f32)
    inv_p = np.tile(inv, 16)
    ang_s = (np.float32(PAST) * inv).astype(f32)
    csrow = np.stack([np.tile(np.cos(ang_s), 16), np.tile(np.sin(ang_s), 16)]).astype(f32)
    c_idx = np.arange(128)[:, None]
    a_idx = np.arange(128)[None, :]
    m_diag = (a_idx >= c_idx).astype(f32)
    m_next = (a_idx <= c_idx).astype(f32)
    ident = np.eye(128, dtype=f32)
    sel = np.zeros((16, 16, 128), f32)
    eb = np.zeros((128, 16, 16), f32)
    for b in range(16):
        sel[b, b, :] = 1.0
        eb[:, b, b] = 1.0
    xT_all = [np.ascontiguousarray(np.pad(x_prompt[b].T, ((0, 0), (AH + HALO, 0)))) for b in range(2)]
    common = dict(
        wain=np.asarray(w_a_in, f32), wagrp=np.asarray(w_a_group, f32), waout=np.asarray(w_a_out, f32),
        wk=np.ascontiguousarray(wk), wv=np.ascontiguousarray(wv), wbq=np.ascontiguousarray(wbq), wbg=wbg,
        wbout=np.ascontiguousarray(np.asarray(w_b_out, f32)[:, sw, :]), vcol=np.ascontiguousarray(vcol), vrow=np.ascontiguousarray(vrow), csrow=csrow,
        ident=ident, sel=sel.reshape(16, 16 * 128), eb=eb.reshape(128, 256),
    )
    in_maps = []
    for core in range(NCORES):
        b, i = divmod(core, 4)
        c0 = OWN * i
        pos = (c0 - (AH + HALO) + np.arange(EXT)).astype(np.int64)
        posf = pos.astype(f32)
        ang = inv_p[:, None] * posf[None, :]
        invc = np.stack([1.0 / np.where(pos >= 0, np.minimum(pos + 1, w), w).astype(f32) for w in (2, 4, 8, 16)]).astype(f32)
        halo_valid = 1.0 if i > 0 else 0.0
        mask = np.concatenate([m_diag, m_next, m_next * halo_valid], axis=1).astype(f32)
        m = dict(common)
        m["xT"] = np.ascontiguousarray(xT_all[b][:, c0:c0 + EXT].reshape(8, 128, EXT))
        m["invc"] = invc
        m["cosT"] = np.cos(ang).astype(f32)
        m["sinT"] = np.sin(ang).astype(f32)
        m["mask"] = mask
        bs = slice(16 * core, 16 * core + 16)
        m["xs"] = np.ascontiguousarray(np.asarray(x_sample, f32)[bs, 0, :])
        m["st"] = np.ascontiguousarray(np.asarray(state_pool, f32)[:, bs])
        for g, d in enumerate(DILS):
            m["ck%d" % g] = np.ascontiguousarray(caches[g][bs, 0:128 * d:d].reshape(16, 128, 2, D))
        in_maps.append(m)

    nc = build_program()
    res = run_bass_kernel_spmd(nc, in_maps, core_ids=list(range(NCORES)))
    R = res.results

    y_prompt = np.empty((2, SEQ, D), f32)
    y_sample = np.empty((128, 1, D), f32)
    pool_prompt = np.empty((2, 2, 15, D), f32)
    pool_sample = np.empty((2, 128, 15, D), f32)
    kvp = [np.empty((2, 128 * d, 2, 16, 64), f32) for d in DILS]
    kvs = [np.empty((128, 1, 2, 16, 64), f32) for _ in DILS]
    for core in range(NCORES):
        b, i = divmod(core, 4)
        r = R[core]
        y_prompt[b, OWN * i:OWN * (i + 1), :] = r["yT"].reshape(D, OWN).T
        bs = slice(16 * core, 16 * core + 16)
        y_sample[bs, 0, :] = r["ys"]
        pool_sample[:, bs] = r["pools"]
        for g in range(3):
            kvs[g][bs, 0, 0] = r["ks"][:, g].reshape(16, 16, 64)
            kvs[g][bs, 0, 1] = r["vs"][:, g].reshape(16, 16, 64)
        if i == 3:
            for l in range(2):
                pool_prompt[l, b] = r["poolp"][l].reshape(D, 15).T
            for g, d in enumerate(DILS):
                keep = 128 * d
                kT = r["kT%d" % g]
                kT = kT[:, :, kT.shape[2] - keep:]
                k = kT.reshape(8, 16, 8, keep).transpose(3, 1, 0, 2).reshape(keep, 16, 64)
                v = r["v%d" % g]
                v = v[v.shape[0] - keep:].reshape(keep, 16, 64)
                kvp[g][b, :, 0] = k
                kvp[g][b, :, 1] = v
    return (y_prompt, y_sample, pool_prompt, pool_sample,
            kvp[0], kvs[0], kvp[1], kvs[1], kvp[2], kvs[2])
```

```python
import math
from contextlib import ExitStack

import numpy as np
import concourse.bass as bass
import concourse.mybir as mybir
from concourse.bass_utils import run_bass_kernel_spmd

F32 = mybir.dt.float32
BF16 = mybir.dt.bfloat16
ALU = mybir.AluOpType
AF = mybir.ActivationFunctionType
AX = mybir.AxisListType

NCORES = 8
D = 1024
SEQ = 8192
OWN = 2048
HALO = 2048
AH = 512
EXT = AH + HALO + OWN
PAST = 2048
EPS = 1e-6
DILS = (1, 4, 16)
ROPE_THETA = 500000.0
NTA = 256
NT = 512
import os
STOP = int(os.environ.get('KSTOP', '99'))


class _Rec:
    def __init__(self):
        self.calls = []

    def __getattr__(self, name):
        def f(*a, **kw):
            self.calls.append((name, a, kw))
            return self
        return f


class Sched:
    ENGS = ("pe", "act", "dve", "pool", "sp")

    def __init__(self, nc, n_dma_sems=32):
        self.nc = nc
        self.streams = {e: [] for e in self.ENGS}
        self.cnt = {e: 0 for e in self.ENGS}
        self.dma_cnt = {e: 0 for e in self.ENGS}
        self.n_dma_sems = n_dma_sems
        self.last_w = {}
        self.readers = {}
        self.known = {e: {} for e in self.ENGS}
        self.sem_max = {}
        self.final_tokens = []

    def _need(self, eng, tok, waits):
        sem, v = tok
        if self.known[eng].get(sem, 0) >= v:
            return
        self.known[eng][sem] = v
        waits[sem] = max(waits.get(sem, 0), v)

    def op(self, eng, fn, reads=(), writes=(), dma=False, final=False):
        waits = {}
        for k in reads:
            w = self.last_w.get(k)
            if w is not None:
                self._need(eng, w, waits)
        for k in writes:
            w = self.last_w.get(k)
            if w is not None:
                self._need(eng, w, waits)
            for sem, v in self.readers.get(k, {}).items():
                self._need(eng, (sem, v), waits)
        if dma:
            j = self.dma_cnt[eng]
            self.dma_cnt[eng] += 1
            sem = "d_%s_%d" % (eng, j % self.n_dma_sems)
            use = j // self.n_dma_sems
            if use > 0:
                self._need(eng, (sem, 16 * use), waits)
            tok = (sem, 16 * (use + 1))
            inc = 16
        else:
            self.cnt[eng] += 1
            sem = "c_%s" % eng
            tok = (sem, self.cnt[eng])
            inc = 1
        self.sem_max[sem] = tok[1]
        rec = _Rec()
        fn(rec)
        assert rec.calls
        self.streams[eng].append((sorted(waits.items()), rec.calls, sem, inc))
        for k in writes:
            self.last_w[k] = tok
            self.readers[k] = {}
        for k in reads:
            d = self.readers.setdefault(k, {})
            d[tok[0]] = max(d.get(tok[0], 0), tok[1])
        if final:
            self.final_tokens.append(tok)
        return tok

    def barrier(self):
        for eng in self.ENGS:
            waits = {}
            for sem, v in self.sem_max.items():
                self._need(eng, (sem, v), waits)
            if waits:
                self.streams[eng].append((sorted(waits.items()), None, None, 0))

    def emit(self, stack):
        nc = self.nc
        sems = {}
        for name in sorted(self.sem_max):
            sems[name] = stack.enter_context(nc.semaphore(name))
        block = stack.enter_context(nc.Block())
        fin = dict(self.sem_max)
        streams = self.streams

        def run(engname, e, extra_final=False):
            for waits, fn, sem, inc in streams[engname]:
                for s, v in waits:
                    e.wait_ge(sems[s], v)
                if fn is None:
                    continue
                ins = None
                for name, a, kw in fn:
                    ins = getattr(e, name)(*a, **kw)
                ins.then_inc(sems[sem], inc)
            if extra_final:
                for s, v in sorted(fin.items()):
                    e.wait_ge(sems[s], v)

        @block.sync
        def _(e):
            run("sp", e, extra_final=True)

        @block.tensor
        def _(e):
            run("pe", e)

        @block.scalar
        def _(e):
            run("act", e)

        @block.vector
        def _(e):
            run("dve", e)

        @block.gpsimd
        def _(e):
            run("pool", e)


class Arena:
    def __init__(self, t, cols):
        self.t = t
        self.cols = cols
        self.off = 0

    def f32(self, n):
        a = self.off
        self.off += n
        self.hw = max(getattr(self, "hw", 0), self.off)
        assert self.off <= self.cols, ("arena overflow", self.off, self.cols)
        return self.t[:, a:a + n]

    def bf16(self, n):
        m = (n + 1) // 2
        a = self.off
        self.off += m
        self.hw = max(getattr(self, "hw", 0), self.off)
        assert self.off <= self.cols, ("arena overflow", self.off, self.cols)
        return self.t[:, a:a + m].bitcast(BF16)[:, 0:n]

    def mark(self):
        return self.off

    def reset(self, m):
        self.off = m


def build_program():
    nc = bass.Bass("TRN2", target_bir_lowering=False)

    def din(name, shape):
        return nc.dram_tensor(name, list(shape), F32, kind="ExternalInput").ap()

    def dout(name, shape):
        return nc.dram_tensor(name, list(shape), F32, kind="ExternalOutput").ap()

    def dscr(name, shape, dt):
        return nc.dram_tensor(name, list(shape), dt, kind="Internal").ap()

    xT_h = din("xT", [8, 128, EXT])
    invc_h = din("invc", [4, EXT])
    cos_h = din("cosT", [128, EXT])
    sin_h = din("sinT", [128, EXT])
    xs_h = din("xs", [16, D])
    st_h = din("st", [2, 16, 15, D])
    ck_h = [din("ck%d" % g, [16, 128, 2, D]) for g in range(3)]
    wain_h = din("wain", [2, D, 2048])
    wagrp_h = din("wagrp", [2, 4, 256, 256])
    waout_h = din("waout", [2, D, D])
    wk_h = din("wk", [D, 3072])
    wv_h = din("wv", [D, 3072])
    wbq_h = din("wbq", [2, D, 3072])
    wbg_h = din("wbg", [2, D, D])
    wbout_h = din("wbout", [2, D, D])
    vcol_h = din("vcol", [128, 64])
    vrow_h = din("vrow", [8, D])
    csrow_h = din("csrow", [2, 128])
    mask_h = din("mask", [128, 384])
    ident_h = din("ident", [128, 128])
    sel_h = din("sel", [16, 16 * 128])
    eb_h = din("eb", [128, 16 * 16])

    yT_o = dout("yT", [8, 128, OWN])
    ys_o = dout("ys", [16, D])
    poolp_o = dout("poolp", [2, 8, 128, 15])
    pools_o = dout("pools", [2, 16, 15, D])
    kT_o = [dout("kT0", [8, 128, 512]), dout("kT1", [8, 128, 512]), dout("kT2", [8, 128, OWN])]
    v_o = [dout("v0", [512, D]), dout("v1", [512, D]), dout("v2", [OWN, D])]
    ks_o = dout("ks", [16, 3, D])
    vs_o = dout("vs", [16, 3, D])

    H = dscr("H", [8, 128, EXT], F32)
    KTall = dscr("KTall", [8, 128, 8832], BF16)
    VS = [dscr("VS%d" % g, [8, HALO + OWN, 128], BF16) for g in range(3)]
    QTall = dscr("QTall", [8, 128, 3 * OWN], BF16)

    S = Sched(nc)
    with ExitStack() as st:
        ACOLS = int(os.environ.get("ACOLS", "52900"))
        arena_t = st.enter_context(nc.sbuf_tensor("arena", [128, ACOLS], F32))
        A = Arena(arena_t, ACOLS)
        PS = [st.enter_context(nc.psum_tensor("ps%d" % i, [128, 512], F32)) for i in range(8)]
        PSK = ["ps%d" % i for i in range(8)]

        vcol = A.f32(64)
        ident = A.f32(128)
        identb = A.bf16(128)
        ones_bf = A.bf16(128)
        mask = A.bf16(384)
        hs = A.f32(D)
        csrow = A.f32(256)
        eb = A.bf16(256)
        s_g = A.f32(D)
        P0 = A.mark()

        S.op("sp", lambda e: e.dma_start(out=vcol, in_=vcol_h), writes=["vcol"], dma=True)
        S.op("sp", lambda e: e.dma_start(out=ident, in_=ident_h), writes=["ident"], dma=True)
        S.op("pool", lambda e: e.dma_start(out=identb, in_=ident_h), writes=["identb"], dma=True)
        S.op("pool", lambda e: e.dma_start(out=mask, in_=mask_h), writes=["mask"], dma=True)
        S.op("pool", lambda e: e.dma_start(out=eb, in_=eb_h), writes=["eb"], dma=True)
        S.op("pool", lambda e: e.memset(ones_bf, 1.0), writes=["ones"])
        S.op("sp", lambda e: e.dma_start(out=hs[0:16, :], in_=xs_h), writes=["hs"], dma=True)
        S.op("sp", lambda e: e.dma_start(out=csrow[0:16, :].rearrange("p (v n) -> p v n", v=2),
                                         in_=csrow_h.unsqueeze(0).broadcast_to([16, 2, 128])),
             writes=["csrow"], dma=True)

        VGA, VAS, VGKV, VGB, VGF = 0, 2, 4, 5, 7

        def gcol(vi, k):
            return vcol[:, vi * 8 + k: vi * 8 + k + 1]

        def load_grow(vi):
            S.op("sp", lambda e: e.dma_start(out=s_g[0:16, :], in_=vrow_h[vi:vi + 1, :].broadcast_to([16, D])),
                 writes=["s_g"], dma=True)
            return s_g[0:16, :]

        psrot = [0]

        def next_ps(lo=0, hi=7):
            i = lo + psrot[0] % (hi - lo)
            psrot[0] += 1
            return PS[i], PSK[i]

        def mm_group(out_ap, pairs, reads, wkey):
            def fn(e):
                ins = None
                n = len(pairs)
                for i, (l, r) in enumerate(pairs):
                    ins = e.matmul(out_ap, l, r, start=(i == 0), stop=(i == n - 1))
                return ins
            S.op("pe", fn, reads=reads, writes=[wkey])

        def rms_fm(h3, n, vi, hsq3, hn3, rstd, pfx, spfx=None, hnpfx=None):
            sp_ = pfx if spfx is None else spfx
            hp_ = pfx if hnpfx is None else hnpfx
            for k in range(8):
                S.op("act", lambda e, k=k: e.activation(out=hsq3[:, k, :], in_=h3[:, k, :], func=AF.Square),
                     reads=[pfx + "h%d" % k], writes=[sp_ + "hsq%d" % k])
            mm_group(PS[7][:, 0:n], [(ones_bf, hsq3[:, k, :]) for k in range(8)],
                     ["ones"] + [sp_ + "hsq%d" % k for k in range(8)], PSK[7])
            S.op("dve", lambda e: e.tensor_scalar(out=rstd, in0=PS[7][:, 0:n], scalar1=1.0 / D, scalar2=EPS,
                                                  op0=ALU.mult, op1=ALU.add), reads=[PSK[7]], writes=[sp_ + "rstd"])
            S.op("act", lambda e: e.activation(out=rstd, in_=rstd, func=AF.Sqrt),
                 reads=[sp_ + "rstd"], writes=[sp_ + "rstd"])
            S.op("dve", lambda e: e.reciprocal(out=rstd, in_=rstd), reads=[sp_ + "rstd"], writes=[sp_ + "rstd"])
            for k in range(8):
                S.op("dve", lambda e, k=k: e.scalar_tensor_tensor(out=hn3[:, k, :], in0=h3[:, k, :], scalar=gcol(vi, k),
                                                                  in1=rstd, op0=ALU.mult, op1=ALU.mult),
                     reads=[pfx + "h%d" % k, sp_ + "rstd", "vcol"], writes=[hp_ + "hn%d" % k])

        def s_rms(src, vi, dst, tmp, col):
            gr = load_grow(vi)
            S.op("dve", lambda e: e.tensor_tensor(out=tmp, in0=src, in1=src, op=ALU.mult), reads=["hs"], writes=["s_tmp"])
            S.op("dve", lambda e: e.tensor_reduce(out=col, in_=tmp, axis=AX.X, op=ALU.add), reads=["s_tmp"], writes=["s_col"])
            S.op("dve", lambda e: e.tensor_scalar(out=col, in0=col, scalar1=1.0 / D, scalar2=EPS, op0=ALU.mult, op1=ALU.add),
                 reads=["s_col"], writes=["s_col"])
            S.op("act", lambda e: e.activation(out=col, in_=col, func=AF.Sqrt), reads=["s_col"], writes=["s_col"])
            S.op("dve", lambda e: e.reciprocal(out=col, in_=col), reads=["s_col"], writes=["s_col"])
            S.op("dve", lambda e: e.scalar_tensor_tensor(out=dst, in0=src, scalar=col, in1=gr, op0=ALU.mult, op1=ALU.mult),
                 reads=["hs", "s_col", "s_g"], writes=["s_hn"])

        def s_T(src, skey, dstT, dkey):
            def fn(e):
                ins = None
                for k in range(8):
                    ins = e.transpose(PS[7][:, k * 16:(k + 1) * 16], src[:, k * 128:(k + 1) * 128], ident[0:16, 0:16])
                return ins
            S.op("pe", fn, reads=[skey, "ident"], writes=[PSK[7]])
            S.op("act", lambda e: e.activation(out=dstT, in_=PS[7][:, 0:128], func=AF.Copy), reads=[PSK[7]], writes=[dkey])

        def s_proj(hT, hkey, w3, wkey, col0, ncols, bank, bkey):
            hT3 = hT.rearrange("p (k t) -> p k t", k=8)
            mm_group(PS[bank][0:16, 0:ncols], [(hT3[:, k, :], w3[:, k, col0:col0 + ncols]) for k in range(8)],
                     [hkey, wkey], bkey)

        KSs = dscr("KSs", [16, 3 * D], F32)
        VSs = dscr("VSs", [16, 3 * D], F32)
        QSs = dscr("QSs", [16, 3 * D], F32)
        SGs = dscr("SGs", [16, D], F32)
        hs16 = hs[0:16, :]
        rot = {"kbf": 0, "k32": 0, "v": 0}

        for l in range(2):
            A.reset(P0)
            wain = A.bf16(8 * 2048).rearrange("p (k n) -> p k n", k=8)
            wagrp = A.bf16(4 * 2 * 256).rearrange("p (g k n) -> p g k n", g=4, k=2)
            waout = A.bf16(8 * D).rearrange("p (k n) -> p k n", k=8)
            S.op("pool", lambda e, l=l: e.dma_start(out=wain, in_=wain_h[l].rearrange("(k p) n -> p k n", p=128)),
                 writes=["wain"], dma=True)
            S.op("pool", lambda e, l=l: e.dma_start(out=wagrp, in_=wagrp_h[l].rearrange("g (k p) n -> p g k n", p=128)),
                 writes=["wagrp"], dma=True)
            S.op("pool", lambda e, l=l: e.dma_start(out=waout, in_=waout_h[l].rearrange("(k p) n -> p k n", p=128)),
                 writes=["waout"], dma=True)
            PAW = A.mark()
            h3s = [A.f32(8 * NT).rearrange("p (k n) -> p k n", k=8) for _ in range(2)]
            hsq3 = A.bf16(8 * NT).rearrange("p (k n) -> p k n", k=8)
            hn3 = A.bf16(8 * NT).rearrange("p (k n) -> p k n", k=8)
            rstd = A.f32(NT)
            UW = 15 + NT
            uexts = [A.f32(8 * UW).rearrange("p (k n) -> p k n", k=8) for _ in range(2)]
            carry = [A.f32(8 * 15).rearrange("p (k n) -> p k n", k=8) for _ in range(3)]
            tA = A.f32(UW)
            tB = A.f32(UW)
            sg3s = [A.bf16(8 * NT).rearrange("p (k n) -> p k n", k=8) for _ in range(2)]
            r3 = A.bf16(8 * NT).rearrange("p (k n) -> p k n", k=8)
            y3 = A.bf16(8 * NT).rearrange("p (k n) -> p k n", k=8)
            invc3s = [A.f32(4 * NT).rearrange("p (g n) -> p g n", g=4) for _ in range(2)]
            ntilesA = EXT // NT
            src_h = xT_h if l == 0 else H

            def a_load_sq(t, l=l, src_h=src_h):
                p_ = t % 2
                pfx = "A%d" % p_
                c0 = t * NT
                h3 = h3s[p_]
                S.op("sp", lambda e: e.dma_start(out=h3, in_=src_h[:, :, c0:c0 + NT].rearrange("k p n -> p k n")),
                     reads=["H"], writes=[pfx + "h%d" % k for k in range(8)], dma=True)
                S.op("sp", lambda e: e.dma_start(out=invc3s[p_], in_=invc_h[:, c0:c0 + NT].unsqueeze(0).broadcast_to([128, 4, NT])),
                     writes=[pfx + "invc"], dma=True)
                for k in range(8):
                    S.op("act", lambda e, k=k: e.activation(out=hsq3[:, k, :], in_=h3[:, k, :], func=AF.Square),
                         reads=[pfx + "h%d" % k], writes=["Ahsq%d" % k])
                mm_group(PS[7][:, 0:NT], [(ones_bf, hsq3[:, k, :]) for k in range(8)],
                         ["ones"] + ["Ahsq%d" % k for k in range(8)], PSK[7])

            def a_rstd(t):
                S.op("dve", lambda e: e.tensor_scalar(out=rstd, in0=PS[7][:, 0:NT], scalar1=1.0 / D, scalar2=EPS,
                                                      op0=ALU.mult, op1=ALU.add), reads=[PSK[7]], writes=["Arstd"])
                S.op("act", lambda e: e.activation(out=rstd, in_=rstd, func=AF.Sqrt), reads=["Arstd"], writes=["Arstd"])
                S.op("dve", lambda e: e.reciprocal(out=rstd, in_=rstd), reads=["Arstd"], writes=["Arstd"])

            def a_hn(t, l=l):
                p_ = t % 2
                pfx = "A%d" % p_
                h3 = h3s[p_]
                for k in range(8):
                    S.op("dve", lambda e, k=k: e.scalar_tensor_tensor(out=hn3[:, k, :], in0=h3[:, k, :], scalar=gcol(VGA + l, k),
                                                                      in1=rstd, op0=ALU.mult, op1=ALU.mult),
                         reads=[pfx + "h%d" % k, "Arstd", "vcol"], writes=["Ahn%d" % k])

            def a_inproj(t):
                p_ = t % 2
                pfx = "A%d" % p_
                uext, sg3 = uexts[p_], sg3s[p_]
                for oc in range(16):
                    ps, pk = next_ps()
                    mm_group(ps[:, 0:NT], [(wain[:, k, oc * 128:(oc + 1) * 128], hn3[:, k, :]) for k in range(8)],
                             ["wain"] + ["Ahn%d" % k for k in range(8)], pk)
                    if oc < 8:
                        S.op("act", lambda e, ps=ps, oc=oc: e.activation(out=uext[:, oc, 15:15 + NT], in_=ps[:, 0:NT], func=AF.Copy),
                             reads=[pk], writes=[pfx + "ue%d" % oc])
                    else:
                        S.op("act", lambda e, ps=ps, oc=oc: e.activation(out=sg3[:, oc - 8, :], in_=ps[:, 0:NT], func=AF.Silu),
                             reads=[pk], writes=[pfx + "sg%d" % (oc - 8)])
                S.op("pool", lambda e: e.tensor_copy(out=carry[t % 3], in_=uext[:, :, NT:NT + 15]),
                     reads=[pfx + "ue%d" % k for k in range(8)], writes=["carry%d" % (t % 3)])

            def a_stageB(t, part, l=l):
                p_ = t % 2
                pfx = "A%d" % p_
                c0 = t * NT
                h3, uext, sg3, invc3 = h3s[p_], uexts[p_], sg3s[p_], invc3s[p_]
                uks = [pfx + "ue%d" % k for k in range(8)]
                if part == "pool":
                    a_b_pool(t, p_, pfx, c0, h3, uext, sg3, invc3, uks)
                elif part == "z":
                    a_b_z(t, p_, pfx, c0, h3, uext, sg3, invc3, uks)
                else:
                    a_b_out(t, p_, pfx, c0, h3, uext, sg3, invc3, uks)

            def a_b_pool(t, p_, pfx, c0, h3, uext, sg3, invc3, uks, l=l):
                if t == 0:
                    S.op("pool", lambda e: e.memset(uext[:, :, 0:15], 0.0), reads=uks, writes=uks)
                else:
                    S.op("pool", lambda e: e.tensor_copy(out=uext[:, :, 0:15], in_=carry[(t - 1) % 3]),
                         reads=["carry%d" % ((t - 1) % 3)] + uks, writes=uks)
                for oc in range(8):
                    g = oc // 2
                    uk = pfx + "ue%d" % oc
                    src = uext[:, oc, :]
                    eng = "dve"
                    ta, tb = (tA, "tA"), (tB, "tB")
                    cur, curk = src, uk
                    lo = 0
                    for lev in range(g + 1):
                        sh = 1 << lev
                        dst, dk = ta if lev % 2 == 0 else tb
                        nlo = lo + sh
                        S.op(eng, lambda e, dst=dst, cur=cur, nlo=nlo, sh=sh: e.tensor_tensor(
                            out=dst[:, nlo:UW], in0=cur[:, nlo:UW], in1=cur[:, nlo - sh:UW - sh], op=ALU.add),
                            reads=[curk], writes=[dk])
                        cur, curk, lo = dst, dk, nlo
                    oth, othk = tb if curk == ta[1] else ta
                    if c0 <= AH + HALO < c0 + NT:
                        S.op(eng, lambda e, cur=cur, oth=oth, g=g: e.tensor_tensor(out=oth[:, 15:UW], in0=cur[:, 15:UW], in1=invc3[:, g, :], op=ALU.mult),
                             reads=[curk, pfx + "invc"], writes=[othk])
                        S.op(eng, lambda e, oth=oth, src=src, oc=oc: e.tensor_tensor(out=r3[:, oc, :], in0=oth[:, 15:UW], in1=src[:, 15:UW], op=ALU.subtract),
                             reads=[othk, uk], writes=["Ar%d" % oc])
                    else:
                        S.op(eng, lambda e, cur=cur, src=src, oc=oc, g=g: e.scalar_tensor_tensor(
                            out=r3[:, oc, :], in0=cur[:, 15:UW], scalar=1.0 / (2 ** (g + 1)), in1=src[:, 15:UW], op0=ALU.mult, op1=ALU.subtract),
                            reads=[curk, uk], writes=["Ar%d" % oc])
                    if t == ntilesA - 1:
                        S.op("sp", lambda e, oc=oc: e.dma_start(out=poolp_o[l, oc], in_=uext[:, oc, NT:NT + 15]),
                             reads=[uk], dma=True, final=True)

            def a_b_z(t, p_, pfx, c0, h3, uext, sg3, invc3, uks, l=l):
                for oc in range(8):
                    g = oc // 2
                    ps, pk = next_ps()
                    mm_group(ps[:, 0:NT], [(wagrp[:, g, kk, (oc % 2) * 128:(oc % 2 + 1) * 128], r3[:, 2 * g + kk, :]) for kk in range(2)],
                             ["wagrp", "Ar%d" % (2 * g), "Ar%d" % (2 * g + 1)], pk)
                    S.op("dve", lambda e, ps=ps, oc=oc: e.scalar_tensor_tensor(out=y3[:, oc, :], in0=ps[:, 0:NT], scalar=gcol(VAS + l, oc),
                                                                          in1=sg3[:, oc, :], op0=ALU.mult, op1=ALU.mult),
                         reads=[pk, pfx + "sg%d" % oc, "vcol"], writes=["Ay%d" % oc])

            def a_b_out(t, p_, pfx, c0, h3, uext, sg3, invc3, uks, l=l):
                for dc in range(8):
                    ps, pk = next_ps()
                    mm_group(ps[:, 0:NT], [(waout[:, k, dc * 128:(dc + 1) * 128], y3[:, k, :]) for k in range(8)],
                             ["waout"] + ["Ay%d" % k for k in range(8)], pk)
                    S.op("dve", lambda e, ps=ps, dc=dc: e.tensor_tensor(out=h3[:, dc, :], in0=h3[:, dc, :], in1=ps[:, 0:NT], op=ALU.add),
                         reads=[pk, pfx + "h%d" % dc], writes=[pfx + "h%d" % dc])
                S.op("sp", lambda e: e.dma_start(out=H[:, :, c0:c0 + NT].rearrange("k p n -> p k n"), in_=h3),
                     reads=[pfx + "h%d" % k for k in range(8)], writes=["Hst"], dma=True)

            a_load_sq(0)
            a_rstd(0)
            a_hn(0)
            a_inproj(0)
            for t in range(ntilesA):
                nxt = t + 1 < ntilesA
                if nxt:
                    a_load_sq(t + 1)
                a_stageB(t, "pool")
                if nxt:
                    a_rstd(t + 1)
                a_stageB(t, "z")
                if nxt:
                    a_hn(t + 1)
                a_stageB(t, "out")
                if nxt:
                    a_inproj(t + 1)

            S.barrier()
            A.reset(PAW)
            s_tmp = A.f32(D)[0:16, :]
            s_col = A.f32(1)[0:16, :]
            s_hn = A.f32(D)[0:16, :]
            s_hT = A.bf16(128)
            s_us = A.f32(D)[0:16, :]
            s_sg = A.f32(D)[0:16, :]
            s_r = A.f32(D)[0:16, :]
            s_y = A.f32(D)[0:16, :]
            s_rT = A.bf16(128)
            s_yT = A.bf16(128)
            s_st = A.f32(15 * 256)[0:16, :]
            s_rms(hs16, VGA + l, s_hn, s_tmp, s_col)
            s_T(s_hn, "s_hn", s_hT, "s_hT")
            for q in range(4):
                s_proj(s_hT, "s_hT", wain, "wain", q * 512, 512, q, PSK[q])
            for q in range(2):
                S.op("act", lambda e, q=q: e.activation(out=s_us[:, q * 512:(q + 1) * 512], in_=PS[q][0:16, :], func=AF.Copy),
                     reads=[PSK[q]], writes=["s_us"])
                S.op("act", lambda e, q=q: e.activation(out=s_sg[:, q * 512:(q + 1) * 512], in_=PS[2 + q][0:16, :], func=AF.Silu),
                     reads=[PSK[2 + q]], writes=["s_sg"])
            S.op("sp", lambda e, l=l: e.dma_start(out=pools_o[l, :, 14, :], in_=s_us), reads=["s_us"], dma=True, final=True)
            S.op("sp", lambda e, l=l: e.dma_start(out=pools_o[l, :, 0:14, :], in_=st_h[l, :, 1:15, :]), dma=True, final=True)
            for g in range(4):
                w = 2 ** (g + 1)
                sl = slice(g * 256, (g + 1) * 256)
                stv = s_st[:, 0:(w - 1) * 256].rearrange("p (j c) -> p j c", c=256)
                S.op("sp", lambda e, stv=stv, w=w, sl=sl, l=l: e.dma_start(out=stv, in_=st_h[l, :, 15 - (w - 1):15, sl]),
                     writes=["s_st"], dma=True)
                S.op("dve", lambda e, stv=stv, sl=sl: e.tensor_reduce(out=s_r[:, sl], in_=stv.rearrange("p j c -> p c j"), axis=AX.X, op=ALU.add),
                     reads=["s_st"], writes=["s_r"])
                S.op("dve", lambda e, sl=sl: e.tensor_tensor(out=s_r[:, sl], in0=s_r[:, sl], in1=s_us[:, sl], op=ALU.add),
                     reads=["s_r", "s_us"], writes=["s_r"])
                S.op("dve", lambda e, sl=sl, w=w: e.scalar_tensor_tensor(out=s_r[:, sl], in0=s_r[:, sl], scalar=1.0 / w, in1=s_us[:, sl],
                                                                    op0=ALU.mult, op1=ALU.subtract),
                     reads=["s_r", "s_us"], writes=["s_r"])
            s_T(s_r, "s_r", s_rT, "s_rT")
            rT3 = s_rT.rearrange("p (k t) -> p k t", k=8)
            for g in range(4):
                bank = g // 2
                mm_group(PS[bank][0:16, (g % 2) * 256:(g % 2 + 1) * 256],
                         [(rT3[:, 2 * g + kk, :], wagrp[:, g, kk, :]) for kk in range(2)], ["s_rT", "wagrp"], PSK[bank])
            asr = load_grow(VAS + l)
            for q in range(2):
                sl = slice(q * 512, (q + 1) * 512)
                S.op("dve", lambda e, q=q, sl=sl: e.tensor_tensor(out=s_y[:, sl], in0=PS[q][0:16, :], in1=asr[:, sl], op=ALU.mult),
                     reads=[PSK[q], "s_g"], writes=["s_y"])
            S.op("dve", lambda e: e.tensor_tensor(out=s_y, in0=s_y, in1=s_sg, op=ALU.mult), reads=["s_y", "s_sg"], writes=["s_y"])
            s_T(s_y, "s_y", s_yT, "s_yT")
            for q in range(2):
                s_proj(s_yT, "s_yT", waout, "waout", q * 512, 512, q, PSK[q])
                sl = slice(q * 512, (q + 1) * 512)
                S.op("dve", lambda e, q=q, sl=sl: e.tensor_tensor(out=hs16[:, sl], in0=hs16[:, sl], in1=PS[q][0:16, :], op=ALU.add),
                     reads=[PSK[q], "hs"], writes=["hs"])
            S.barrier()

        if STOP == 1:
            S.emit(st)
            return nc
        A.reset(P0)

        wk = A.bf16(8 * 3072).rearrange("p (k n) -> p k n", k=8)
        wv = A.bf16(8 * 3072).rearrange("p (k n) -> p k n", k=8)
        S.op("pool", lambda e: e.dma_start(out=wk, in_=wk_h.rearrange("(k p) n -> p k n", p=128)), writes=["wk"], dma=True)
        S.op("pool", lambda e: e.dma_start(out=wv, in_=wv_h.rearrange("(k p) n -> p k n", p=128)), writes=["wv"], dma=True)
        PKW = A.mark()
        h3s = [A.f32(8 * NT).rearrange("p (k n) -> p k n", k=8) for _ in range(2)]
        hsq3 = A.bf16(8 * NT).rearrange("p (k n) -> p k n", k=8)
        hn3s = [A.bf16(8 * NT).rearrange("p (k n) -> p k n", k=8) for _ in range(2)]
        rstd = A.f32(NT)
        cosTs = [A.f32(NT) for _ in range(2)]
        sinTs = [A.f32(NT) for _ in range(2)]
        h3, hn3, cosT, sinT = h3s[0], hn3s[0], cosTs[0], sinTs[0]
        kx = [A.f32(NT), A.f32(NT)]
        rt = [A.f32(NT) for _ in range(4)]
        kr = [A.f32(NT), A.f32(NT)]
        kbf = [A.bf16(NT) for _ in range(5)]
        k32 = [A.f32(NT) for _ in range(1)]
        vbf = [A.bf16(D) for _ in range(3)]
        v32 = [A.f32(D) for _ in range(1)]

        def rope_fm(x0, x1, k0, k1, pfx, n=NT):
            S.op("dve", lambda e: e.tensor_tensor(out=rt[0][:, 0:n], in0=x0, in1=cosT[:, 0:n], op=ALU.mult), reads=[k0, pfx + "cs"], writes=["rt0"])
            S.op("pool", lambda e: e.tensor_tensor(out=rt[1][:, 0:n], in0=x1, in1=sinT[:, 0:n], op=ALU.mult), reads=[k1, pfx + "cs"], writes=["rt1"])
            S.op("dve", lambda e: e.tensor_tensor(out=rt[2][:, 0:n], in0=x0, in1=sinT[:, 0:n], op=ALU.mult), reads=[k0, pfx + "cs"], writes=["rt2"])
            S.op("pool", lambda e: e.tensor_tensor(out=rt[3][:, 0:n], in0=x1, in1=cosT[:, 0:n], op=ALU.mult), reads=[k1, pfx + "cs"], writes=["rt3"])
            S.op("dve", lambda e: e.tensor_tensor(out=kr[0][:, 0:n], in0=rt[0][:, 0:n], in1=rt[1][:, 0:n], op=ALU.subtract), reads=["rt0", "rt1"], writes=["kr0"])
            S.op("pool", lambda e: e.tensor_tensor(out=kr[1][:, 0:n], in0=rt[2][:, 0:n], in1=rt[3][:, 0:n], op=ALU.add), reads=["rt2", "rt3"], writes=["kr1"])

        def fm_qk_chunks(w3, wkey, g, pfx, scale, dst_fn, need32, out32_fn):
            for c in range(8):
                ps, pk = next_ps()
                col = g * 1024 + c * 128
                mm_group(ps[:, 0:NT], [(w3[:, k, col:col + 128], hn3[:, k, :]) for k in range(8)],
                         [wkey] + [pfx + "hn%d" % k for k in range(8)], pk)
                if c < 2:
                    S.op("act", lambda e, ps=ps, c=c: e.activation(out=kx[c], in_=ps[:, 0:NT], func=AF.Copy), reads=[pk], writes=["kx%d" % c])
                    if c == 1:
                        rope_fm(kx[0], kx[1], "kx0", "kx1", pfx)
                        for cc in range(2):
                            i = rot["kbf"] % len(kbf)
                            rot["kbf"] += 1
                            S.op("act", lambda e, i=i, cc=cc: e.activation(out=kbf[i], in_=kr[cc], func=AF.Copy, scale=scale),
                                 reads=["kr%d" % cc], writes=["kbf%d" % i])
                            dst_fn(cc, kbf[i], "kbf%d" % i)
                            if need32:
                                out32_fn(cc, kr[cc], "kr%d" % cc)
                else:
                    i = rot["kbf"] % len(kbf)
                    rot["kbf"] += 1
                    S.op("act", lambda e, ps=ps, i=i: e.activation(out=kbf[i], in_=ps[:, 0:NT], func=AF.Copy, scale=scale),
                         reads=[pk], writes=["kbf%d" % i])
                    dst_fn(c, kbf[i], "kbf%d" % i)
                    if need32:
                        j = rot["k32"] % len(k32)
                        rot["k32"] += 1
                        S.op("dve", lambda e, ps=ps, j=j: e.tensor_copy(out=k32[j], in_=ps[:, 0:NT]), reads=[pk], writes=["k32%d" % j])
                        out32_fn(c, k32[j], "k32%d" % j)

        def kv_load_h(t):
            pb_ = t % 2
            pfx = "K%d" % pb_
            e0 = AH + t * NT
            S.op("sp", lambda e: e.dma_start(out=h3s[pb_], in_=H[:, :, e0:e0 + NT].rearrange("k p n -> p k n")),
                 reads=["H"], writes=[pfx + "h%d" % k for k in range(8)], dma=True)

        def kv_stageA(t):
            pb_ = t % 2
            pfx = "K%d" % pb_
            e0 = AH + t * NT
            S.op("sp", lambda e: e.dma_start(out=cosTs[pb_], in_=cos_h[:, e0:e0 + NT]), writes=[pfx + "cs"], dma=True)
            S.op("sp", lambda e: e.dma_start(out=sinTs[pb_], in_=sin_h[:, e0:e0 + NT]), writes=[pfx + "cs"], dma=True)
            rms_fm(h3s[pb_], NT, VGKV, hsq3, hn3s[pb_], rstd, pfx, "K")

        kv_load_h(0)
        kv_load_h(1)
        kv_stageA(0)
        for t in range(8):
            e0 = AH + t * NT
            k0 = t * NT
            own = t >= 4
            groups = [2] if t < 3 else [0, 1, 2]
            if t + 2 < 8:
                kv_load_h(t + 2)
            if t + 1 < 8:
                kv_stageA(t + 1)
            KP = "K%d" % (t % 2)
            h3, hn3, cosT, sinT = h3s[t % 2], hn3s[t % 2], cosTs[t % 2], sinTs[t % 2]
            for g in groups:
                need32 = own and (g == 2 or t == 7)
                ocol = ((t - 4) * NT if g == 2 else 0) if need32 else 0

                def dst_fn(c, tile, key, g=g, k0=k0):
                    tok0 = HALO - 128 * DILS[g]
                    lo = max(k0, tok0)
                    if lo >= k0 + NT:
                        return
                    koff = (0, 2176, 2176 + 2560)[g]
                    S.op("sp", lambda e: e.dma_start(out=KTall[c, :, koff + lo - tok0:koff + k0 + NT - tok0], in_=tile[:, lo - k0:NT]),
                         reads=[key], writes=["KT"], dma=True)

                def out32_fn(c, tile, key, g=g, ocol=ocol):
                    S.op("sp", lambda e: e.dma_start(out=kT_o[g][c, :, ocol:ocol + NT], in_=tile), reads=[key], dma=True, final=True)

                fm_qk_chunks(wk, "wk", g, KP, 1.0, dst_fn, need32, out32_fn)
                for sub in range(4):
                    j = rot["v"] % len(vbf)
                    rot["v"] += 1
                    for half in range(2):
                        ps, pk = next_ps()
                        col = g * 1024 + half * 512
                        mm_group(ps[:, 0:512], [(hn3[:, k, sub * 128:(sub + 1) * 128], wv[:, k, col:col + 512]) for k in range(8)],
                                 ["wv"] + [KP + "hn%d" % k for k in range(8)], pk)
                        S.op("act", lambda e, ps=ps, j=j, half=half: e.activation(out=vbf[j][:, half * 512:(half + 1) * 512], in_=ps[:, 0:512], func=AF.Copy),
                             reads=[pk], writes=["vbf%d" % j])
                        if need32:
                            S.op("dve", lambda e, ps=ps, half=half, j=j: e.tensor_copy(out=v32[0][:, half * 512:(half + 1) * 512], in_=ps[:, 0:512]),
                                 reads=[pk], writes=["v32"])
                    r0 = k0 + sub * 128
                    S.op("sp", lambda e, j=j, g=g, r0=r0: e.dma_start(out=VS[g][:, r0:r0 + 128, :].rearrange("hp n d -> n hp d"),
                                                                 in_=vbf[j].rearrange("p (hp d) -> p hp d", hp=8)),
                         reads=["vbf%d" % j], writes=["VS"], dma=True)
                    if need32:
                        orow = ocol + sub * 128
                        S.op("sp", lambda e, g=g, orow=orow, j=j: e.dma_start(out=v_o[g][orow:orow + 128, :], in_=v32[0]),
                             reads=["v32"], dma=True, final=True)

        S.barrier()
        A.reset(PKW)
        s_tmp = A.f32(D)[0:16, :]
        s_col = A.f32(1)[0:16, :]
        s_hn = A.f32(D)[0:16, :]
        s_hT = A.bf16(128)
        s_kd = A.f32(D)[0:16, :]
        s_t = [A.f32(128)[0:16, :] for _ in range(4)]
        ksn3 = A.f32(3 * D)[0:16, :].rearrange("p (g n) -> p g n", g=3)
        vsn3 = A.f32(3 * D)[0:16, :].rearrange("p (g n) -> p g n", g=3)
        cs16 = csrow[0:16, 0:128]
        sn16 = csrow[0:16, 128:256]

        def s_rope_perm(src, skey, dst, dkey, scale):
            x0 = src[:, 0:128]
            x1 = src[:, 128:256]
            S.op("dve", lambda e: e.tensor_tensor(out=s_t[0], in0=x0, in1=cs16, op=ALU.mult), reads=[skey, "csrow"], writes=["s_t0"])
            S.op("dve", lambda e: e.tensor_tensor(out=s_t[1], in0=x1, in1=sn16, op=ALU.mult), reads=[skey, "csrow"], writes=["s_t1"])
            S.op("dve", lambda e: e.tensor_tensor(out=s_t[2], in0=x0, in1=sn16, op=ALU.mult), reads=[skey, "csrow"], writes=["s_t2"])
            S.op("dve", lambda e: e.tensor_tensor(out=s_t[3], in0=x1, in1=cs16, op=ALU.mult), reads=[skey, "csrow"], writes=["s_t3"])
            S.op("dve", lambda e: e.tensor_tensor(out=x0, in0=s_t[0], in1=s_t[1], op=ALU.subtract), reads=["s_t0", "s_t1"], writes=[skey])
            S.op("dve", lambda e: e.tensor_tensor(out=x1, in0=s_t[2], in1=s_t[3], op=ALU.add), reads=["s_t2", "s_t3"], writes=[skey])
            S.op("act", lambda e: e.activation(out=dst.rearrange("p (h c i) -> p h c i", h=16, c=8),
                                               in_=src.rearrange("p (c h i) -> p h c i", c=8, h=16), func=AF.Copy, scale=scale),
                 reads=[skey], writes=[dkey])

        s_rms(hs16, VGKV, s_hn, s_tmp, s_col)
        s_T(s_hn, "s_hn", s_hT, "s_hT")
        for g in range(3):
            for q in range(2):
                s_proj(s_hT, "s_hT", wk, "wk", g * 1024 + q * 512, 512, q, PSK[q])
                s_proj(s_hT, "s_hT", wv, "wv", g * 1024 + q * 512, 512, 2 + q, PSK[2 + q])
                sl = slice(q * 512, (q + 1) * 512)
                S.op("act", lambda e, q=q, sl=sl: e.activation(out=s_kd[:, sl], in_=PS[q][0:16, :], func=AF.Copy), reads=[PSK[q]], writes=["s_kd"])
                S.op("act", lambda e, q=q, sl=sl, g=g: e.activation(out=vsn3[:, g, sl], in_=PS[2 + q][0:16, :], func=AF.Copy), reads=[PSK[2 + q]], writes=["vsn"])
            s_rope_perm(s_kd, "s_kd", ksn3[:, g, :], "ksn", 1.0)
        S.op("sp", lambda e: e.dma_start(out=ks_o.rearrange("p g n -> p (g n)"), in_=ksn3.rearrange("p g n -> p (g n)")), reads=["ksn"], dma=True, final=True)
        S.op("sp", lambda e: e.dma_start(out=vs_o.rearrange("p g n -> p (g n)"), in_=vsn3.rearrange("p g n -> p (g n)")), reads=["vsn"], dma=True, final=True)
        S.op("sp", lambda e: e.dma_start(out=KSs, in_=ksn3.rearrange("p g n -> p (g n)")), reads=["ksn"], writes=["KSs"], dma=True)
        S.op("sp", lambda e: e.dma_start(out=VSs, in_=vsn3.rearrange("p g n -> p (g n)")), reads=["vsn"], writes=["VSs"], dma=True)

        S.barrier()
        if STOP == 2:
            S.emit(st)
            return nc
        A.reset(P0)

        GEOM = []
        for g, d in enumerate(DILS):
            GEOM.append((d, OWN // (128 * d), 128 * d))
        KW = [GEOM[g][2] + OWN for g in range(3)]
        KOFF = [0, KW[0], KW[0] + KW[1]]
        NBLK = [(GEOM[g][1] + 1) * GEOM[g][0] for g in range(3)]
        BOFF = [0, NBLK[0], NBLK[0] + NBLK[1]]

        sgB = A.bf16(8 * OWN).rearrange("p (k n) -> p k n", k=8)
        PSG = A.mark()
        yTB = A.bf16(8 * OWN).rearrange("p (k n) -> p k n", k=8)
        PB1 = A.mark()
        for j in range(2):
            A.reset(PSG)
            wbq = A.bf16(8 * 3072).rearrange("p (k n) -> p k n", k=8)
            wbg = A.bf16(8 * D).rearrange("p (k n) -> p k n", k=8)
            S.op("pool", lambda e, j=j: e.dma_start(out=wbq, in_=wbq_h[j].rearrange("(k p) n -> p k n", p=128)), writes=["wbq"], dma=True)
            S.op("pool", lambda e, j=j: e.dma_start(out=wbg, in_=wbg_h[j].rearrange("(k p) n -> p k n", p=128)), writes=["wbg"], dma=True)
            PBW = A.mark()
            h3s = [A.f32(8 * NT).rearrange("p (k n) -> p k n", k=8) for _ in range(2)]
            hsq3 = A.bf16(8 * NT).rearrange("p (k n) -> p k n", k=8)
            hn3s = [A.bf16(8 * NT).rearrange("p (k n) -> p k n", k=8) for _ in range(2)]
            rstd = A.f32(NT)
            cosTs = [A.f32(NT) for _ in range(2)]
            sinTs = [A.f32(NT) for _ in range(2)]
            kx = [A.f32(NT), A.f32(NT)]
            rt = [A.f32(NT) for _ in range(4)]
            kr = [A.f32(NT), A.f32(NT)]
            kbf = [A.bf16(NT) for _ in range(5)]

            def b1_load_h(t):
                pb_ = t % 2
                pfx = "B%d" % pb_
                e0 = AH + HALO + t * NT
                S.op("sp", lambda e: e.dma_start(out=h3s[pb_], in_=H[:, :, e0:e0 + NT].rearrange("k p n -> p k n")),
                     reads=["H"], writes=[pfx + "h%d" % k for k in range(8)], dma=True)

            def b1_stageA(t, j=j):
                pb_ = t % 2
                pfx = "B%d" % pb_
                e0 = AH + HALO + t * NT
                S.op("sp", lambda e: e.dma_start(out=cosTs[pb_], in_=cos_h[:, e0:e0 + NT]), writes=[pfx + "cs"], dma=True)
                S.op("sp", lambda e: e.dma_start(out=sinTs[pb_], in_=sin_h[:, e0:e0 + NT]), writes=[pfx + "cs"], dma=True)
                rms_fm(h3s[pb_], NT, VGB + j, hsq3, hn3s[pb_], rstd, pfx, "B")

            b1_load_h(0)
            b1_load_h(1)
            b1_stageA(0)
            for t in range(4):
                q0 = t * NT
                if t + 2 < 4:
                    b1_load_h(t + 2)
                if t + 1 < 4:
                    b1_stageA(t + 1)
                BP = "B%d" % (t % 2)
                h3, hn3, cosT, sinT = h3s[t % 2], hn3s[t % 2], cosTs[t % 2], sinTs[t % 2]
                for g in range(3):
                    def dst_fn(c, tile, key, g=g, q0=q0):
                        S.op("sp", lambda e: e.dma_start(out=QTall[c, :, g * OWN + q0:g * OWN + q0 + NT], in_=tile), reads=[key], writes=["QT"], dma=True)
                    fm_qk_chunks(wbq, "wbq", g, BP, 0.125, dst_fn, False, None)
                for oc in range(8):
                    ps, pk = next_ps()
                    mm_group(ps[:, 0:NT], [(wbg[:, k, oc * 128:(oc + 1) * 128], hn3[:, k, :]) for k in range(8)],
                             ["wbg"] + [BP + "hn%d" % k for k in range(8)], pk)
                    S.op("act", lambda e, ps=ps, oc=oc, q0=q0: e.activation(out=sgB[:, oc, q0:q0 + NT], in_=ps[:, 0:NT], func=AF.Silu),
                         reads=[pk], writes=["sgB%d" % oc])
            S.barrier()
            A.reset(PBW)
            s_tmp = A.f32(D)[0:16, :]
            s_col = A.f32(1)[0:16, :]
            s_hn = A.f32(D)[0:16, :]
            s_hT = A.bf16(128)
            s_kd = A.f32(D)[0:16, :]
            s_t = [A.f32(128)[0:16, :] for _ in range(4)]
            qsn3 = A.f32(3 * D)[0:16, :].rearrange("p (g n) -> p g n", g=3)
            sgs = A.f32(D)[0:16, :]
            s_rms(hs16, VGB + j, s_hn, s_tmp, s_col)
            s_T(s_hn, "s_hn", s_hT, "s_hT")
            for g in range(3):
                for q in range(2):
                    s_proj(s_hT, "s_hT", wbq, "wbq", g * 1024 + q * 512, 512, q, PSK[q])
                    sl = slice(q * 512, (q + 1) * 512)
                    S.op("act", lambda e, q=q, sl=sl: e.activation(out=s_kd[:, sl], in_=PS[q][0:16, :], func=AF.Copy), reads=[PSK[q]], writes=["s_kd"])
                s_rope_perm(s_kd, "s_kd", qsn3[:, g, :], "qsn", 0.125)
            for q in range(2):
                s_proj(s_hT, "s_hT", wbg, "wbg", q * 512, 512, 2 + q, PSK[2 + q])
                sl = slice(q * 512, (q + 1) * 512)
                S.op("act", lambda e, q=q, sl=sl: e.activation(out=sgs[:, sl], in_=PS[2 + q][0:16, :], func=AF.Silu), reads=[PSK[2 + q]], writes=["sgs"])
            S.op("sp", lambda e: e.dma_start(out=QSs, in_=qsn3.rearrange("p g n -> p (g n)")), reads=["qsn"], writes=["QSs"], dma=True)
            S.op("sp", lambda e: e.dma_start(out=SGs, in_=sgs), reads=["sgs"], writes=["SGs"], dma=True)

            S.barrier()
            if STOP == 3:
                S.emit(st)
                return nc
            A.reset(PB1)

            KTh = [A.bf16(8832) for _ in range(2)]
            Vu = [A.bf16(NBLK[g_] * 256).rearrange("p (b x) -> p b x", x=256) for g_ in range(3)]
            QTh = [A.bf16(3 * OWN) for _ in range(2)]
            NPT = 6
            PT = [A.bf16(256) for _ in range(NPT)]
            ACC = [A.f32(OWN), A.f32(OWN)]
            REC = A.f32(OWN)
            for i in range(3):
                S.op("pool", lambda e, i=i: e.memset(Vu[i], 1.0), writes=["Vu%d" % i])
            SB = [0, 1, 2, 3]
            OB = [4, 5, 6, 7]
            LAGP = 2
            tiles = []
            orot = [0]
            ucount = 0
            for hp in range(8):
                for g in range(3):
                    d, NB, halo = GEOM[g]
                    ub = ucount % 3
                    ucount += 1
                    first_of_unit = True
                    for r in range(d):
                        okeys = {}
                        for kb in range(-1, NB):
                            for hh in range(2):
                                qbs = [qb for qb in (kb, kb + 1) if 0 <= qb < NB]
                                kc0 = halo + kb * 128 * d + r
                                q0c = qbs[0] * 128 * d + r
                                nq = 128 * len(qbs)
                                if kb == -1:
                                    mk = mask[:, 256:384]
                                elif kb == NB - 1:
                                    mk = mask[:, 0:128]
                                else:
                                    mk = mask[:, 0:256]
                                pvs = []
                                for qi, qb in enumerate(qbs):
                                    if (hh, qb) not in okeys:
                                        okeys[(hh, qb)] = OB[orot[0] % len(OB)]
                                        orot[0] += 1
                                    a0 = qb * 128 * d + r
                                    pvs.append((qi, okeys[(hh, qb)], kb == qb - 1, kb == qb, slice(a0, a0 + 127 * d + 1, d)))
                                tiles.append(dict(hp=hp, g=g, hh=hh, ub=ub, load=first_of_unit, load_hp=(first_of_unit and g == 0),
                                                  kcols=slice(kc0, kc0 + 127 * d + 1, d),
                                                  qcols=slice(q0c, q0c + (nq - 1) * d + 1, d), nq=nq, mk=mk,
                                                  blk=r * (NB + 1) + (kb + 1), pvs=pvs,
                                                  last_of_head=(g == 2 and r == d - 1 and kb == NB - 1)))
                                first_of_unit = False

            def emit_loads(T):
                hp, g, ub = T["hp"], T["g"], T["ub"]
                hb = hp % 2
                d, NB, halo = GEOM[g]
                tok0 = HALO - halo
                if T["load_hp"]:
                    for hh in range(2):
                        for c in range(8):
                            p0 = hh * 64 + c * 8
                            hrow = (2 * hp + hh) * 8
                            S.op("sp", lambda e, p0=p0, hrow=hrow, c=c: e.dma_start(
                                out=KTh[hb][p0:p0 + 8, :], in_=KTall[c, hrow:hrow + 8, :]),
                                reads=["KT"], writes=["KTh%d" % hb], dma=True)
                            S.op("sp", lambda e, p0=p0, hrow=hrow, c=c: e.dma_start(
                                out=QTh[hb][p0:p0 + 8, :], in_=QTall[c, hrow:hrow + 8, :]),
                                reads=["QT"], writes=["QTh%d" % hb], dma=True)
                for r in range(d):
                    b0 = r * (NB + 1)
                    S.op("sp", lambda e, r=r, b0=b0: e.dma_start(
                        out=Vu[ub][:, b0:b0 + NB + 1, 64:192],
                        in_=VS[g][hp, tok0 + r:HALO + OWN:d, :].rearrange("(kb a) x -> a kb x", a=128)),
                        reads=["VS"], writes=["Vu%d" % ub], dma=True)

            def emit_front(i, T):
                if T["load"]:
                    emit_loads(T)
                ub, pb, nq = T["ub"], 64 * T["hh"], T["nq"]
                si = SB[i % len(SB)]
                pt, ptk = PT[i % NPT], "PT%d" % (i % NPT)
                hb = T["hp"] % 2
                g_ = T["g"]
                kc, qc = T["kcols"], T["qcols"]
                kcols = slice(KOFF[g_] + kc.start, KOFF[g_] + kc.stop, kc.step)
                qcols = slice(g_ * OWN + qc.start, g_ * OWN + qc.stop, qc.step)
                S.op("pe", lambda e: e.matmul(PS[si][:, 0:nq], KTh[hb][pb:pb + 64, kcols], QTh[hb][pb:pb + 64, qcols],
                                              start=True, stop=True),
                     reads=["KTh%d" % hb, "QTh%d" % hb], writes=[PSK[si]])
                S.op("act", lambda e: e.activation(out=pt[:, 0:nq], in_=PS[si][:, 0:nq], func=AF.Exp), reads=[PSK[si]], writes=[ptk])
                S.op("dve" if i % 3 == 2 else "pool", lambda e: e.tensor_tensor(out=pt[:, 0:nq], in0=pt[:, 0:nq], in1=T["mk"], op=ALU.mult),
                     reads=[ptk, "mask"], writes=[ptk])

            def emit_back(i, T):
                ub, hh, g, hp = T["ub"], T["hh"], T["g"], T["hp"]
                pt, ptk = PT[i % NPT], "PT%d" % (i % NPT)
                lhsT = Vu[ub][:, T["blk"], 128 * hh:128 * hh + 128]
                acc, acck = ACC[hh], "ACC%d" % hh
                for (qi, oi, is_first, is_last, acols) in T["pvs"]:
                    S.op("pe", lambda e, qi=qi, oi=oi, is_first=is_first, is_last=is_last: e.matmul(
                        PS[oi][:, 0:128], lhsT, pt[:, qi * 128:(qi + 1) * 128], start=is_first, stop=is_last),
                        reads=[ptk, "Vu%d" % ub], writes=[PSK[oi]])
                    if is_last:
                        if g == 0:
                            S.op("act", lambda e, oi=oi, acols=acols: e.activation(out=acc[:, acols], in_=PS[oi][:, 0:128], func=AF.Copy),
                                 reads=[PSK[oi]], writes=[acck])
                        else:
                            S.op("dve", lambda e, oi=oi, acols=acols: e.tensor_tensor(out=acc[:, acols], in0=acc[:, acols], in1=PS[oi][:, 0:128], op=ALU.add),
                                 reads=[PSK[oi], acck], writes=[acck])
                if T["last_of_head"]:
                    nb = 64 - 64 * hh
                    db = 64 - nb
                    S.op("dve", lambda e: e.reciprocal(out=REC[nb:nb + 64, :], in_=acc[db:db + 64, :]), reads=[acck], writes=["REC"])
                    S.op("dve", lambda e: e.tensor_tensor(out=REC[nb:nb + 64, :], in0=REC[nb:nb + 64, :], in1=acc[nb:nb + 64, :], op=ALU.mult),
                         reads=[acck, "REC"], writes=["REC"])
                    S.op("pool", lambda e: e.tensor_tensor(out=yTB[nb:nb + 64, hp, :], in0=REC[nb:nb + 64, :], in1=sgB[nb:nb + 64, hp, :], op=ALU.mult),
                         reads=["REC", "sgB%d" % hp], writes=["yTB%d" % hp])

            ntl = len(tiles)
            assert ntl % 2 == 0
            npairs = ntl // 2
            for jp in range(npairs + LAGP):
                if jp < npairs:
                    emit_front(2 * jp, tiles[2 * jp])
                    emit_front(2 * jp + 1, tiles[2 * jp + 1])
                if jp >= LAGP:
                    emit_back(2 * (jp - LAGP), tiles[2 * (jp - LAGP)])
                    emit_back(2 * (jp - LAGP) + 1, tiles[2 * (jp - LAGP) + 1])

            S.barrier()
            if STOP == 4:
                S.emit(st)
                return nc
            A.reset(PB1)

            s_yT = A.bf16(128)
            PB3 = A.mark()
            Kc = [A.f32(D) for _ in range(3)]
            Vc = [A.f32(D) for _ in range(3)]
            prod = A.f32(D)
            sc = A.f32(16)
            p32 = A.f32(16)
            PVt = [A.bf16(1040) for _ in range(2)]
            PVn = A.bf16(1040)[0:16, :]
            prodn = A.f32(D)[0:16, :]
            scn = A.f32(16)[0:16, :]
            pn32 = A.f32(16)[0:16, :]
            recs = A.f32(16)[0:16, :]
            comb = A.f32(D)[0:16, :]
            sel = A.bf16(16 * 128)
            qb16 = A.bf16(3 * D)[0:16, :].rearrange("p (g n) -> p g n", g=3)
            qsn3 = A.f32(3 * D)[0:16, :].rearrange("p (g n) -> p g n", g=3)
            ksn3 = A.f32(3 * D)[0:16, :].rearrange("p (g n) -> p g n", g=3)
            vsn3 = A.f32(3 * D)[0:16, :].rearrange("p (g n) -> p g n", g=3)
            sgs = A.f32(D)[0:16, :]
            S.op("pool", lambda e: e.dma_start(out=sel[0:16, :], in_=sel_h), writes=["sel"], dma=True)
            S.op("sp", lambda e: e.dma_start(out=qsn3.rearrange("p g n -> p (g n)"), in_=QSs), reads=["QSs"], writes=["qsn"], dma=True)
            S.op("sp", lambda e: e.dma_start(out=ksn3.rearrange("p g n -> p (g n)"), in_=KSs), reads=["KSs"], writes=["ksn"], dma=True)
            S.op("sp", lambda e: e.dma_start(out=vsn3.rearrange("p g n -> p (g n)"), in_=VSs), reads=["VSs"], writes=["vsn"], dma=True)
            S.op("sp", lambda e: e.dma_start(out=sgs, in_=SGs), reads=["SGs"], writes=["sgs"], dma=True)
            S.op("act", lambda e: e.activation(out=qb16, in_=qsn3, func=AF.Copy), reads=["qsn"], writes=["qb16"])
            eb3 = eb.rearrange("p (b m) -> p b m", b=16)
            sel3 = sel[0:16, :].rearrange("p (b m) -> p b m", b=16)
            ACCB = (PS[5], PS[6], PS[7])
            ACCK = (PSK[5], PSK[6], PSK[7])

            def acc_mm(lhsT, rhs, rkeys, first, last):
                def fn(e):
                    e.matmul(ACCB[0][0:16, 0:512], lhsT, rhs[:, 0:512], start=first, stop=last)
                    e.matmul(ACCB[1][0:16, 0:512], lhsT, rhs[:, 512:1024], start=first, stop=last)
                    return e.matmul(ACCB[2][0:16, 0:16], lhsT, rhs[:, 1024:1040], start=first, stop=last)
                S.op("pe", fn, reads=rkeys, writes=list(ACCK))

            nsteps = 3 + 48
            step = 0
            for g in range(3):
                S.op("dve", lambda e, g=g: e.tensor_tensor(out=prodn, in0=qsn3[:, g, :], in1=ksn3[:, g, :], op=ALU.mult),
                     reads=["qsn", "ksn"], writes=["prodn"])
                S.op("dve", lambda e: e.tensor_reduce(out=scn, in_=prodn.rearrange("p (h d) -> p h d", h=16), axis=AX.X, op=ALU.add),
                     reads=["prodn"], writes=["scn"])
                S.op("act", lambda e: e.activation(out=pn32, in_=scn, func=AF.Exp), reads=["scn"], writes=["pn32"])
                S.op("act", lambda e: e.activation(out=PVn[:, 1024:1040], in_=pn32, func=AF.Copy), reads=["pn32"], writes=["PVn"])
                S.op("dve", lambda e, g=g: e.tensor_tensor(out=PVn[:, 0:1024].rearrange("p (h d) -> p h d", h=16),
                                                       in0=vsn3[:, g, :].rearrange("p (h d) -> p h d", h=16),
                                                       in1=pn32.unsqueeze(2).broadcast_to([16, 16, 64]), op=ALU.mult),
                     reads=["vsn", "pn32", "PVn"], writes=["PVn"])
                step += 1
                acc_mm(identb[0:16, 0:16], PVn, ["PVn", "identb"], step == 1, False)
            sc2 = [sc, A.f32(16)]
            p322 = [p32, A.f32(16)]
            its = [(b, g) for b in range(16) for g in range(3)]

            def s_front(n_, b, g):
                i = n_ % 2
                ik = n_ % 3
                S.op("sp", lambda e: e.dma_start(out=Kc[ik], in_=ck_h[g][b, :, 0, :]), writes=["Kc%d" % ik], dma=True)
                S.op("sp", lambda e: e.dma_start(out=Vc[ik], in_=ck_h[g][b, :, 1, :]), writes=["Vc%d" % ik], dma=True)
                for q in range(2):
                    bk = 2 * i + q
                    S.op("pe", lambda e, q=q, bk=bk: e.matmul(PS[bk][:, 0:512], sel3[:, b, :], qb16[:, g, q * 512:(q + 1) * 512], start=True, stop=True),
                         reads=["sel", "qb16"], writes=[PSK[bk]])
                    S.op("dve", lambda e, q=q, bk=bk: e.tensor_tensor(out=prod[:, q * 512:(q + 1) * 512], in0=Kc[ik][:, q * 512:(q + 1) * 512], in1=PS[bk][:, 0:512], op=ALU.mult),
                         reads=[PSK[bk], "Kc%d" % ik], writes=["prod"])
                S.op("dve", lambda e: e.tensor_reduce(out=sc2[i], in_=prod.rearrange("p (h d) -> p h d", h=16), axis=AX.X, op=ALU.add),
                     reads=["prod"], writes=["sc%d" % i])
                S.op("act", lambda e: e.activation(out=p322[i], in_=sc2[i], func=AF.Exp), reads=["sc%d" % i], writes=["p32%d" % i])
                S.op("act", lambda e: e.activation(out=PVt[i][:, 1024:1040], in_=p322[i], func=AF.Copy), reads=["p32%d" % i], writes=["PVt%d" % i])
                S.op("pool", lambda e: e.tensor_tensor(out=PVt[i][:, 0:1024].rearrange("p (h d) -> p h d", h=16),
                                                       in0=Vc[ik].rearrange("p (h d) -> p h d", h=16),
                                                       in1=p322[i].unsqueeze(2).broadcast_to([128, 16, 64]), op=ALU.mult),
                     reads=["Vc%d" % ik, "p32%d" % i, "PVt%d" % i], writes=["PVt%d" % i])

            def s_back(n_, b, g):
                i = n_ % 2
                acc_mm(eb3[:, b, :], PVt[i], ["PVt%d" % i, "eb"], False, n_ == len(its) - 1)

            for n_ in range(len(its) + 1):
                if n_ < len(its):
                    s_front(n_, *its[n_])
                if n_ >= 1:
                    s_back(n_ - 1, *its[n_ - 1])
            S.op("dve", lambda e: e.reciprocal(out=recs, in_=ACCB[2][0:16, 0:16]), reads=[ACCK[2]], writes=["recs"])
            for q in range(2):
                for ho in range(2):
                    hi = 1 - ho
                    S.op("dve", lambda e, q=q, ho=ho, hi=hi: e.tensor_tensor(
                        out=comb[:, q * 512:(q + 1) * 512].rearrange("p (hp hh d) -> p hp hh d", hp=4, hh=2)[:, :, ho, :],
                        in0=ACCB[q][0:16, 0:512].rearrange("p (hp hh d) -> p hp hh d", hp=4, hh=2)[:, :, hi, :],
                        in1=recs[:, q * 8:(q + 1) * 8].rearrange("p (hp hh) -> p hp hh", hh=2)[:, :, hi].unsqueeze(2).broadcast_to([16, 4, 64]),
                        op=ALU.mult),
                        reads=[ACCK[q], "recs"], writes=["comb"])
            S.op("dve", lambda e: e.tensor_tensor(out=comb, in0=comb, in1=sgs, op=ALU.mult), reads=["comb", "sgs"], writes=["comb"])
            s_T(comb, "comb", s_yT, "s_yT")
            S.barrier()
            if STOP == 5:
                S.emit(st)
                return nc
            A.reset(PB3)

            wbo = A.bf16(8 * D).rearrange("p (k n) -> p k n", k=8)
            S.op("pool", lambda e, j=j: e.dma_start(out=wbo, in_=wbout_h[j].rearrange("(k p) n -> p k n", p=128)), writes=["wbo"], dma=True)
            h3s = [A.f32(8 * NT).rearrange("p (k n) -> p k n", k=8) for _ in range(2)]
            hsq3 = A.bf16(8 * NT).rearrange("p (k n) -> p k n", k=8)
            rstd = A.f32(NT)
            yo3 = A.f32(8 * NT).rearrange("p (k n) -> p k n", k=8)

            def b3_load(t):
                e0 = AH + HALO + t * NT
                S.op("sp", lambda e: e.dma_start(out=h3s[t % 2], in_=H[:, :, e0:e0 + NT].rearrange("k p n -> p k n")),
                     reads=["H"], writes=["O%dh%d" % (t % 2, k) for k in range(8)], dma=True)

            for t in range(4):
                b3_load(t)
                if t == 1:
                    break
            for t in range(4):
                e0 = AH + HALO + t * NT
                q0 = t * NT
                h3 = h3s[t % 2]
                OP = "O%d" % (t % 2)
                for dc in range(8):
                    ps, pk = next_ps()
                    mm_group(ps[:, 0:NT], [(wbo[:, hp, dc * 128:(dc + 1) * 128], yTB[:, hp, q0:q0 + NT]) for hp in range(8)],
                             ["wbo"] + ["yTB%d" % hp for hp in range(8)], pk)
                    S.op("dve", lambda e, ps=ps, dc=dc: e.tensor_tensor(out=h3[:, dc, :], in0=h3[:, dc, :], in1=ps[:, 0:NT], op=ALU.add),
                         reads=[pk, OP + "h%d" % dc], writes=[OP + "h%d" % dc])
                if j == 0:
                    S.op("sp", lambda e, e0=e0: e.dma_start(out=H[:, :, e0:e0 + NT].rearrange("k p n -> p k n"), in_=h3),
                         reads=[OP + "h%d" % k for k in range(8)], writes=["Hout"], dma=True)
                else:
                    for k in range(8):
                        S.op("act", lambda e, k=k: e.activation(out=hsq3[:, k, :], in_=h3[:, k, :], func=AF.Square),
                             reads=[OP + "h%d" % k], writes=["Ohsq%d" % k])
                    mm_group(PS[7][:, 0:NT], [(ones_bf, hsq3[:, k, :]) for k in range(8)], ["ones"] + ["Ohsq%d" % k for k in range(8)], PSK[7])
                    S.op("dve", lambda e: e.tensor_scalar(out=rstd, in0=PS[7][:, 0:NT], scalar1=1.0 / D, scalar2=EPS, op0=ALU.mult, op1=ALU.add),
                         reads=[PSK[7]], writes=["Orstd"])
                    S.op("act", lambda e: e.activation(out=rstd, in_=rstd, func=AF.Sqrt), reads=["Orstd"], writes=["Orstd"])
                    S.op("dve", lambda e: e.reciprocal(out=rstd, in_=rstd), reads=["Orstd"], writes=["Orstd"])
                    for k in range(8):
                        S.op("dve", lambda e, k=k: e.scalar_tensor_tensor(out=yo3[:, k, :], in0=h3[:, k, :], scalar=gcol(VGF, k), in1=rstd,
                                                                      op0=ALU.mult, op1=ALU.mult),
                             reads=[OP + "h%d" % k, "Orstd", "vcol"], writes=["yo%d" % k])
                    S.op("sp", lambda e, q0=q0: e.dma_start(out=yT_o[:, :, q0:q0 + NT].rearrange("k p n -> p k n"), in_=yo3),
                         reads=["yo%d" % k for k in range(8)], dma=True, final=True)
                if t + 2 < 4:
                    b3_load(t + 2)
            for q in range(2):
                s_proj(s_yT, "s_yT", wbo, "wbo", q * 512, 512, q, PSK[q])
                sl = slice(q * 512, (q + 1) * 512)
                S.op("dve", lambda e, q=q, sl=sl: e.tensor_tensor(out=hs16[:, sl], in0=hs16[:, sl], in1=PS[q][0:16, :], op=ALU.add),
                     reads=[PSK[q], "hs"], writes=["hs"])
            if j == 1:
                s_tmp = A.f32(D)[0:16, :]
                s_col = A.f32(1)[0:16, :]
                s_hn = A.f32(D)[0:16, :]
                s_rms(hs16, VGF, s_hn, s_tmp, s_col)
                S.op("sp", lambda e: e.dma_start(out=ys_o, in_=s_hn), reads=["s_hn"], dma=True, final=True)

            S.barrier()

        if os.environ.get('KDEBUG'):
            print('arena high-water', A.hw)
        S.emit(st)
    return nc


def _dim_major_perm():
    idx = np.empty(1024, dtype=np.int64)
    for c in range(8):
        for h in range(16):
            for i in range(8):
                idx[c * 128 + h * 8 + i] = h * 64 + c * 8 + i
    return idx


def kernel(x_prompt, x_sample, state_pool, cache_kv_w128, cache_kv_w512, cache_kv_w2048,
           g_a, w_a_in, w_a_group, a_scale, w_a_out, g_kv, w_kv, g_b, w_b_in, w_b_out, g_final):
    f32 = np.float32
    x_prompt = np.asarray(x_prompt, f32)
    caches = [np.asarray(cache_kv_w128, f32), np.asarray(cache_kv_w512, f32), np.asarray(cache_kv_w2048, f32)]
    perm = _dim_major_perm()
    w_kv = np.asarray(w_kv, f32)
    w_b_in = np.asarray(w_b_in, f32)
    wk = np.concatenate([w_kv[:, (2 * g) * 1024:(2 * g + 1) * 1024][:, perm] for g in range(3)], axis=1)
    wv = np.concatenate([w_kv[:, (2 * g + 1) * 1024:(2 * g + 2) * 1024] for g in range(3)], axis=1)
    wbq = np.stack([np.concatenate([w_b_in[j][:, g * 1024:(g + 1) * 1024][:, perm] for g in range(3)], axis=1) for j in range(2)])
    sw = np.arange(1024).reshape(8, 2, 64)[:, ::-1, :].reshape(1024)
    wbg = np.ascontiguousarray(w_b_in[:, :, 3072:4096][:, :, sw])
    vecs = [g_a[0], g_a[1], a_scale[0], a_scale[1], g_kv, g_b[0], g_b[1], g_final]
    vecs = [np.asarray(v, f32) for v in vecs]
    vrow = np.stack(vecs)
    vcol = np.concatenate([v.reshape(8, 128).T for v in vecs], axis=1)
    inv = (1.0 / (ROPE_THETA ** (np.arange(0, 16, 2, dtype=np.float32) / 16.0))).astype(f32)
    inv_p = np.tile(inv, 16)
    ang_s = (np.float32(PAST) * inv).astype(f32)
    csrow = np.stack([np.tile(np.cos(ang_s), 16), np.tile(np.sin(ang_s), 16)]).astype(f32)
    c_idx = np.arange(128)[:, None]
    a_idx = np.arange(128)[None, :]
    m_diag = (a_idx >= c_idx).astype(f32)
    m_next = (a_idx <= c_idx).astype(f32)
    ident = np.eye(128, dtype=f32)
    sel = np.zeros((16, 16, 128), f32)
    eb = np.zeros((128, 16, 16), f32)
    for b in range(16):
        sel[b, b, :] = 1.0
        eb[:, b, b] = 1.0
    xT_all = [np.ascontiguousarray(np.pad(x_prompt[b].T, ((0, 0), (AH + HALO, 0)))) for b in range(2)]
    common = dict(
        wain=np.asarray(w_a_in, f32), wagrp=np.asarray(w_a_group, f32), waout=np.asarray(w_a_out, f32),
        wk=np.ascontiguousarray(wk), wv=np.ascontiguousarray(wv), wbq=np.ascontiguousarray(wbq), wbg=wbg,
        wbout=np.ascontiguousarray(np.asarray(w_b_out, f32)[:, sw, :]), vcol=np.ascontiguousarray(vcol), vrow=np.ascontiguousarray(vrow), csrow=csrow,
        ident=ident, sel=sel.reshape(16, 16 * 128), eb=eb.reshape(128, 256),
    )
    in_maps = []
    for core in range(NCORES):
        b, i = divmod(core, 4)
        c0 = OWN * i
        pos = (c0 - (AH + HALO) + np.arange(EXT)).astype(np.int64)
        posf = pos.astype(f32)
        ang = inv_p[:, None] * posf[None, :]
        invc = np.stack([1.0 / np.where(pos >= 0, np.minimum(pos + 1, w), w).astype(f32) for w in (2, 4, 8, 16)]).astype(f32)
        halo_valid = 1.0 if i > 0 else 0.0
        mask = np.concatenate([m_diag, m_next, m_next * halo_valid], axis=1).astype(f32)
        m = dict(common)
        m["xT"] = np.ascontiguousarray(xT_all[b][:, c0:c0 + EXT].reshape(8, 128, EXT))
        m["invc"] = invc
        m["cosT"] = np.cos(ang).astype(f32)
        m["sinT"] = np.sin(ang).astype(f32)
        m["mask"] = mask
        bs = slice(16 * core, 16 * core + 16)
        m["xs"] = np.ascontiguousarray(np.asarray(x_sample, f32)[bs, 0, :])
        m["st"] = np.ascontiguousarray(np.asarray(state_pool, f32)[:, bs])
        for g, d in enumerate(DILS):
            m["ck%d" % g] = np.ascontiguousarray(caches[g][bs, 0:128 * d:d].reshape(16, 128, 2, D))
        in_maps.append(m)

    nc = build_program()
    res = run_bass_kernel_spmd(nc, in_maps, core_ids=list(range(NCORES)))
    R = res.results

    y_prompt = np.empty((2, SEQ, D), f32)
    y_sample = np.empty((128, 1, D), f32)
    pool_prompt = np.empty((2, 2, 15, D), f32)
    pool_sample = np.empty((2, 128, 15, D), f32)
    kvp = [np.empty((2, 128 * d, 2, 16, 64), f32) for d in DILS]
    kvs = [np.empty((128, 1, 2, 16, 64), f32) for _ in DILS]
    for core in range(NCORES):
        b, i = divmod(core, 4)
        r = R[core]
        y_prompt[b, OWN * i:OWN * (i + 1), :] = r["yT"].reshape(D, OWN).T
        bs = slice(16 * core, 16 * core + 16)
        y_sample[bs, 0, :] = r["ys"]
        pool_sample[:, bs] = r["pools"]
        for g in range(3):
            kvs[g][bs, 0, 0] = r["ks"][:, g].reshape(16, 16, 64)
            kvs[g][bs, 0, 1] = r["vs"][:, g].reshape(16, 16, 64)
        if i == 3:
            for l in range(2):
                pool_prompt[l, b] = r["poolp"][l].reshape(D, 15).T
            for g, d in enumerate(DILS):
                keep = 128 * d
                kT = r["kT%d" % g]
                kT = kT[:, :, kT.shape[2] - keep:]
                k = kT.reshape(8, 16, 8, keep).transpose(3, 1, 0, 2).reshape(keep, 16, 64)
                v = r["v%d" % g]
                v = v[v.shape[0] - keep:].reshape(keep, 16, 64)
                kvp[g][b, :, 0] = k
                kvp[g][b, :, 1] = v
    return (y_prompt, y_sample, pool_prompt, pool_sample,
            kvp[0], kvs[0], kvp[1], kvs[1], kvp[2], kvs[2])
```
